# Optimizing a Trainium2 kernel written in Bass

```python
import jax, jax.numpy as jnp
from jax import lax
import numpy as np

D_MODEL = 2048
BATCH = 8
SEQ = 2048
DEPTH = 1

CHUNK = 64
SUB_CHUNK = 16
N_SUB = CHUNK // SUB_CHUNK

A_HEADS = 8
A_HEAD_DIM = 128
A_WIDTH = A_HEADS * A_HEAD_DIM
B_HEADS = 4
B_KEY_DIM = 128
B_VAL_DIM = 256
B_KEY_WIDTH = B_HEADS * B_KEY_DIM
B_VAL_WIDTH = B_HEADS * B_VAL_DIM
GK_RANK = 16
GATE_LOGIT_NORMALIZER = 16.0
IN_WIDTH = 4 * A_WIDTH + 2 * B_KEY_WIDTH + 2 * B_VAL_WIDTH + GK_RANK + 2 * D_MODEL
FFN_HIDDEN = -(-(8 * D_MODEL) // (3 * 256)) * 256
N_MOD = 6
EPS = 1e-6

kernel_name = "hgrn2_gla_parallel_hybrid_adaln"


def _in_split_points():
    sizes = [A_WIDTH] * 4 + [B_KEY_WIDTH] * 2 + [B_VAL_WIDTH] * 2 + [GK_RANK, D_MODEL, D_MODEL]
    return [int(v) for v in np.cumsum(sizes)[:-1]]


def rmsnorm(x, w):
    x32 = x.astype(jnp.float32)
    y = x32 * lax.rsqrt(jnp.mean(x32 * x32, axis=-1, keepdims=True) + EPS)
    return (y * w.astype(jnp.float32)).astype(x.dtype)


def to_heads(t, n_heads):
    b, s, hd = t.shape
    return t.reshape(b, s, n_heads, hd // n_heads).transpose(0, 2, 1, 3)


def gated_head_rmsnorm(o, gate, w):
    b, h, t, dv = o.shape
    y = o * lax.rsqrt(jnp.mean(o * o, axis=-1, keepdims=True) + EPS) * w.astype(jnp.float32)
    y = y.transpose(0, 2, 1, 3).reshape(b, t, h * dv)
    return (y * jax.nn.silu(gate.astype(jnp.float32))).astype(gate.dtype)


def chunked_gated_linear_recurrence(q, k, v, g):
    out_dtype = v.dtype
    q, k, v, g = (t.astype(jnp.float32) for t in (q, k, v, g))
    bsz, nh, seq, dk = q.shape
    dv = v.shape[-1]
    nc = seq // CHUNK

    def to_chunks(t):
        return t.reshape(bsz, nh, nc, CHUNK, t.shape[-1]).transpose(2, 0, 1, 3, 4)

    sub_idx = jnp.arange(CHUNK) // SUB_CHUNK
    mask_off = sub_idx[None, :] < jnp.arange(N_SUB)[:, None]
    mask_diag = jnp.tril(jnp.ones((SUB_CHUNK, SUB_CHUNK), dtype=bool))
    eye_sub = jnp.eye(N_SUB, dtype=jnp.float32)

    def step(state, inp):
        qc, kc, vc, gc = inp
        b = jnp.cumsum(gc, axis=2)
        o_inter = jnp.einsum('bhid,bhde->bhie', qc * jnp.exp(b), state)
        bs = b.reshape(bsz, nh, N_SUB, SUB_CHUNK, dk)
        qs = qc.reshape(bsz, nh, N_SUB, SUB_CHUNK, dk)
        ks = kc.reshape(bsz, nh, N_SUB, SUB_CHUNK, dk)
        ref = jnp.concatenate([jnp.zeros_like(bs[:, :, :1, -1]), bs[:, :, :-1, -1]], axis=2)
        q_off = qs * jnp.exp(bs - ref[:, :, :, None, :])
        expo_off = jnp.where(mask_off[None, None, :, :, None],
                             ref[:, :, :, None, :] - b[:, :, None, :, :], -jnp.inf)
        k_off = kc[:, :, None] * jnp.exp(expo_off)
        a_off = jnp.einsum('bhsid,bhsjd->bhsij', q_off, k_off)
        expo_d = jnp.where(mask_diag[None, None, None, :, :, None],
                           bs[:, :, :, :, None, :] - bs[:, :, :, None, :, :], -jnp.inf)
        a_diag = jnp.sum(qs[:, :, :, :, None, :] * ks[:, :, :, None, :, :] * jnp.exp(expo_d), axis=-1)
        a_diag = jnp.einsum('bhsij,sr->bhsirj', a_diag, eye_sub).reshape(bsz, nh, N_SUB, SUB_CHUNK, CHUNK)
        attn = a_off + a_diag
        o_intra = jnp.einsum('bhsij,bhje->bhsie', attn, vc).reshape(bsz, nh, CHUNK, dv)
        b_last = b[:, :, -1]
        k_dec = kc * jnp.exp(b_last[:, :, None, :] - b)
        new_state = jnp.exp(b_last)[..., None] * state + jnp.einsum('bhjd,bhje->bhde', k_dec, vc)
        return new_state, o_inter + o_intra

    state0 = jnp.zeros((bsz, nh, dk, dv), jnp.float32)
    _, o = lax.scan(step, state0, (to_chunks(q), to_chunks(k), to_chunks(v), to_chunks(g)))
    o = o.transpose(1, 2, 0, 3, 4).reshape(bsz, nh, seq, dv)
    return o.astype(out_dtype) if out_dtype != jnp.float32 else o


def hybrid_mixer(h, lower_bound, w_in, w_gk2, b_gk2, a_norm_w, b_norm_w, w_up_a, w_up_b, w_o):
    proj = h @ w_in
    (qa, fa, ia, ga, qb, kb, vb, gb, gk_low, ma, mb) = jnp.split(proj, _in_split_points(), axis=-1)
    f = lower_bound + (1.0 - lower_bound) * jax.nn.sigmoid(fa.astype(jnp.float32))
    o_a = chunked_gated_linear_recurrence(
        to_heads(jax.nn.silu(qa), A_HEADS), to_heads(1.0 - f, A_HEADS),
        to_heads(ia, A_HEADS), to_heads(jnp.log(f), A_HEADS))
    y_a = gated_head_rmsnorm(o_a.astype(jnp.float32), ga, a_norm_w)
    gk = jax.nn.log_sigmoid((gk_low @ w_gk2 + b_gk2).astype(jnp.float32)) / GATE_LOGIT_NORMALIZER
    o_b = chunked_gated_linear_recurrence(
        to_heads(qb, B_HEADS) * (B_KEY_DIM ** -0.5), to_heads(kb, B_HEADS),
        to_heads(vb, B_HEADS), to_heads(gk, B_HEADS))
    y_b = gated_head_rmsnorm(o_b.astype(jnp.float32), gb, b_norm_w)
    merged = jax.nn.sigmoid(ma) * (y_a @ w_up_a) + jax.nn.sigmoid(mb) * (y_b @ w_up_b)
    return merged @ w_o


def swiglu(h, w_ffn_in, w_ffn_out):
    gate, up = jnp.split(h @ w_ffn_in, 2, axis=-1)
    return (jax.nn.silu(gate) * up) @ w_ffn_out


def setup_inputs(seed: int = 0) -> dict:
    key = jax.random.key(seed)
    ks = jax.random.split(key, 20)
    D, L = D_MODEL, DEPTH

    def nrm(k, shape, s):
        return jax.random.normal(k, shape, jnp.float32) * s

    return {
        "x": nrm(ks[0], (BATCH, SEQ, D), 1.0),
        "c": nrm(ks[1], (BATCH, D), 1.0),
        "w_ada": nrm(ks[2], (L, D, N_MOD * D), 0.5 * D ** -0.5),
        "b_ada": nrm(ks[3], (L, N_MOD * D), 0.01),
        "norm1_w": 1.0 + nrm(ks[4], (L, D), 0.02),
        "w_in": nrm(ks[5], (L, D, IN_WIDTH), D ** -0.5),
        "w_gk2": nrm(ks[6], (L, GK_RANK, B_KEY_WIDTH), GK_RANK ** -0.5),
        "b_gk2": nrm(ks[7], (L, B_KEY_WIDTH), 0.01),
        "lb_param": nrm(ks[8], (L + 1, A_WIDTH), 0.5),
        "a_norm_w": 1.0 + nrm(ks[9], (L, A_HEAD_DIM), 0.02),
        "b_norm_w": 1.0 + nrm(ks[10], (L, B_VAL_DIM), 0.02),
        "w_up_a": nrm(ks[11], (L, A_WIDTH, D), A_WIDTH ** -0.5),
        "w_up_b": nrm(ks[12], (L, B_VAL_WIDTH, D), B_VAL_WIDTH ** -0.5),
        "w_o": nrm(ks[13], (L, D, D), D ** -0.5),
        "norm2_w": 1.0 + nrm(ks[14], (L, D), 0.02),
        "w_ffn_in": nrm(ks[15], (L, D, 2 * FFN_HIDDEN), D ** -0.5),
        "w_ffn_out": nrm(ks[16], (L, FFN_HIDDEN, D), FFN_HIDDEN ** -0.5),
        "final_norm_w": 1.0 + nrm(ks[17], (D,), 0.02),
    }


def reference(x, c, w_ada, b_ada, norm1_w, w_in, w_gk2, b_gk2, lb_param, a_norm_w, b_norm_w,
              w_up_a, w_up_b, w_o, norm2_w, w_ffn_in, w_ffn_out, final_norm_w):
    lb_all = jnp.cumsum(jax.nn.softmax(lb_param.astype(jnp.float32), axis=0), axis=0)
    cond = jax.nn.silu(c)
    for l in range(DEPTH):
        mod = (cond @ w_ada[l] + b_ada[l])[:, None, :]
        sh1, sc1, gt1, sh2, sc2, gt2 = jnp.split(mod, N_MOD, axis=-1)
        h = rmsnorm(x, norm1_w[l]) * (1.0 + sc1) + sh1
        x = x + gt1 * hybrid_mixer(h, lb_all[l], w_in[l], w_gk2[l], b_gk2[l], a_norm_w[l], b_norm_w[l],
                                   w_up_a[l], w_up_b[l], w_o[l])
        h = rmsnorm(x, norm2_w[l]) * (1.0 + sc2) + sh2
        x = x + gt2 * swiglu(h, w_ffn_in[l], w_ffn_out[l])
    return rmsnorm(x, final_norm_w)
```

```python
import numpy as np
import concourse.bass as bass
import concourse.mybir as mybir
from concourse.bass_utils import run_bass_kernel_spmd
from contextlib import ExitStack

F32 = mybir.dt.float32
BF16 = mybir.dt.bfloat16
AF = mybir.ActivationFunctionType
ALU = mybir.AluOpType

D = 2048
T = 2048
ST = 1024
NST = T // ST
KC = 16
EPS = 1e-6
UNIT = 512
EPOCH = 30000
NSLOT = 7

SM_BADA = 0
SM_N1 = 96
SM_N2 = 112
SM_NF = 128
SM_LB0 = 144
SM_LB1 = 152
SM_ANW = 160
SM_BNW = 161
SM_BGK = 163
SM_CT = 167
NSM = 183


class Sem:
    def __init__(self, h):
        self.h = h
        self.count = 0


class Tl:
    def __init__(self, base, name, off, ncols, dtype, parts=128, p0=0):
        self.base = base
        self.name = name
        self.off = off
        self.ncols = ncols
        self.dtype = dtype
        self.parts = parts
        self.p0 = p0
        self.esz = 2 if dtype == BF16 else 4
        size = ncols * self.esz
        assert off % 4 == 0 and size % 4 == 0, (off, size)
        ap = base[p0:p0 + parts, off // 4:(off + size) // 4]
        if dtype == BF16:
            ap = ap.bitcast(BF16)
        self.ap = ap
        self.keys = tuple((name, u) for u in range(off // UNIT, (off + size - 1) // UNIT + 1))

    def cols(self, lo, hi, p0=None, parts=None):
        p0 = self.p0 if p0 is None else p0
        parts = self.parts if parts is None else parts
        return Tl(self.base, self.name, self.off + lo * self.esz, hi - lo, self.dtype, parts, p0)


class Tracker:
    def __init__(self, nc, es):
        self.nc = nc
        self.es = es
        self.eng = {'pe': nc.tensor, 'act': nc.scalar, 'dve': nc.vector, 'pool': nc.gpsimd, 'sp': nc.sync}
        self.cnt = {e: 0 for e in self.eng}
        self.esems = {e: [] for e in self.eng}
        self.known = {e: {} for e in self.eng}
        self.lastw = {}
        self.readers = {}
        self.nsem = 0
        self.ninst = 0

    def newsem(self, name):
        self.nsem += 1
        return Sem(self.es.enter_context(self.nc.semaphore(f"{name}_{self.nsem}")))

    def _cursem(self, e):
        ep = self.cnt[e] // EPOCH
        while len(self.esems[e]) <= ep:
            self.esems[e].append(self.newsem(f"e_{e}"))
        return self.esems[e][ep]

    def _wait(self, e, sem, val):
        if self.known[e].get(sem, 0) < val:
            self.eng[e].wait_ge(sem.h, val)
            self.known[e][sem] = val
            self.ninst += 1

    def op(self, e, fns, reads=(), writes=(), dsem=None):
        if not isinstance(fns, (list, tuple)):
            fns = [fns]
        deps = {}

        def add(ev):
            s, v = ev
            if deps.get(s, 0) < v:
                deps[s] = v
        for t in reads:
            for k in t.keys:
                w = self.lastw.get(k)
                if w:
                    add(w)
        for t in writes:
            for k in t.keys:
                w = self.lastw.get(k)
                if w:
                    add(w)
                for ev in self.readers.get(k, {}).items():
                    add(ev)
        own = self._cursem(e) if dsem is None else None
        for s, v in deps.items():
            if e == 'pe' and s in self.esems['pe']:
                continue
            self._wait(e, s, v)
        if dsem is not None and dsem.count > 0:
            self._wait(e, dsem, dsem.count)
        ins = None
        for fn in fns:
            ins = fn(self.eng[e])
            self.ninst += 1
        if dsem is None:
            self.cnt[e] += 1
            local = (self.cnt[e] - 1) % EPOCH + 1
            ins.then_inc(own.h, 1)
            ev = (own, local)
        else:
            dsem.count += 16
            ins.then_inc(dsem.h, 16)
            ev = (dsem, dsem.count)
        for t in reads:
            for k in t.keys:
                r = self.readers.setdefault(k, {})
                if r.get(ev[0], 0) < ev[1]:
                    r[ev[0]] = ev[1]
        for t in writes:
            for k in t.keys:
                self.lastw[k] = ev
                self.readers[k] = {}
        return ev

    def finish(self):
        pass


class _Stop(Exception):
    pass


def build_nc(dbg=None, stop_after=None):
    nc = bass.Bass("TRN2", target_bir_lowering=False)

    def din(name, shape):
        return nc.dram_tensor(name, list(shape), F32, kind="ExternalInput").ap()
    xT = din("xT", [16, 128, T])
    smalls_d = din("smalls", [128, NSM])
    wgk2_d = din("wgk2", [16, 512])
    cst_d = din("cst", [128, 128 + 512 + 1024 + 512])
    w_ada_d = din("w_ada_l", [96, 128, 2048])
    w_in_d = din("w_in_l", [88, 128, 2048])
    w_gkl_d = din("w_gkl", [128, 256])
    w_upa_d = din("w_upa_l", [16, 128, 1024])
    w_upb_d = din("w_upb_l", [16, 128, 1024])
    w_o_d = din("w_o_l", [16, 128, 2048])
    w_fi_d = din("w_fi_l", [88, 128, 2048])
    w_fo_d = din("w_fo_l", [16, 128, 5632])
    outT = nc.dram_tensor("outT", [16, 128, T], F32, kind="ExternalOutput").ap()
    scr_d = nc.dram_tensor("scr", [96 * 128], F32, kind="Internal").ap()
    dbg_out = {}
    if dbg:
        for nm, ncols in dbg.items():
            dbg_out[nm] = nc.dram_tensor("dbg_" + nm, [128, ncols], F32, kind="ExternalOutput").ap()

    with ExitStack() as es:
        tr = Tracker(nc, es)
        KB = 1024
        R1b = es.enter_context(nc.sbuf_tensor("R1", [128, 64 * KB // 4], F32))
        R23b = es.enter_context(nc.sbuf_tensor("R23", [128, 92 * KB // 4], F32))
        WRb = es.enter_context(nc.sbuf_tensor("WR", [128, NSLOT * 4 * KB // 4], F32))
        CSb = es.enter_context(nc.sbuf_tensor("CS", [128, 22 * KB // 4], F32))
        PD = [es.enter_context(nc.psum_tensor(f"PD{i}", [128, 1024], F32)) for i in range(2)]
        PTb = es.enter_context(nc.psum_tensor("PT", [128, 512], F32))
        PAb = es.enter_context(nc.psum_tensor("PA", [128, 512], F32))
        PSb = es.enter_context(nc.psum_tensor("PS", [128, 512], F32))
        POb = es.enter_context(nc.psum_tensor("PO", [128, 512], F32))

        class PsT(Tl):
            def __init__(self, base, name, off, ncols, dtype, parts=128, p0=0):
                self.base, self.name, self.off, self.ncols, self.dtype = base, name, off, ncols, dtype
                self.parts, self.p0 = parts, p0
                self.esz = 2 if dtype == BF16 else 4
                lo = off // self.esz
                self.ap = base[p0:p0 + parts, lo:lo + ncols]
                size = ncols * self.esz
                self.keys = tuple((name, u) for u in range(off // UNIT, (off + size - 1) // UNIT + 1))

            def cols(self, lo, hi, p0=None, parts=None):
                p0 = self.p0 if p0 is None else p0
                parts = self.parts if parts is None else parts
                return PsT(self.base, self.name, self.off + lo * self.esz, hi - lo, self.dtype, parts, p0)

        pd = [PsT(PD[i], f"PD{i}", 0, 1024, F32) for i in range(2)]
        pt = PsT(PTb, "PT", 0, 512, F32)
        pt_bf = PTb[:, :].bitcast(BF16)
        pa = PsT(PAb, "PA", 0, 512, F32)
        ps = PsT(PSb, "PS", 0, 512, F32)
        po = PsT(POb, "PO", 0, 512, F32)

        def r1(off, n, dt):
            return Tl(R1b, "R1", off, n, dt)

        def r23(off, n, dt, parts=128):
            return Tl(R23b, "R23", off, n, dt, parts)

        def cs(off, n, dt, parts=128):
            return Tl(CSb, "CS", off, n, dt, parts)

        sq_pair = r23(80 * 1024, 1024, BF16)
        o = 0
        cm = cs(o, 1024, F32); o += 4096
        m2 = cs(o, 512, BF16); o += 1024
        ident = cs(o, 128, BF16); o += 256
        ones = cs(o, 128, BF16); o += 256
        smalls = cs(o, NSM, F32); o += 768
        mod = cs(o, 96, F32); o += 384
        a1 = cs(o, 16, F32); o += 64
        a2 = cs(o, 16, F32); o += 64
        lb = cs(o, 8, F32); o += 32
        omlb = cs(o, 8, F32); o += 32
        nbgk = cs(o, 4, F32); o += 16
        condb = cs(o, 16, BF16); o += 32
        lbt = cs(o, 8, F32); o += 32
        wgk2 = cs(o, 512, F32, parts=16); o += 2048
        wgkl = cs(o, 256, BF16); o += 512
        persA = [cs(o + h * 512, 128, F32) for h in range(8)]; o += 4096
        persB = [cs(o + h * 1024, 256, F32) for h in range(4)]; o += 4096
        assert o <= 17 * KB + 512

        def sm(c0, n=1):
            return smalls.cols(c0, c0 + n)

        hT = [r1(kc * 2048, 1024, BF16) for kc in range(16)]
        yT = [r1(32 * KB + c * 2048, 1024, BF16) for c in range(16)]
        xres = [r1(fc * 4096, 1024, F32) for fc in range(16)]
        mg = [r23(c * 2048, 1024, BF16) for c in range(16)]
        xall = [r23(fc * 4096, 1024, F32) for fc in range(16)]
        Wb = 32 * KB
        Fw = [r23(Wb + i * 4096, 1024, F32) for i in range(6)]

        xsem = [tr.newsem("x") for _ in range(16)]
        wsem = [tr.newsem("w") for _ in range(NSLOT)]
        csem = [tr.newsem("c") for _ in range(3)]
        osem = [tr.newsem("o") for _ in range(16)]
        wslot = [0]
        pdi = [0]

        def act(out, in_, func, bias=None, scale=None, reads=(), extra_w=()):
            kw = {}
            if bias is not None:
                kw['bias'] = bias.ap if isinstance(bias, Tl) else bias
            if scale is not None:
                kw['scale'] = scale.ap if isinstance(scale, Tl) else scale
            rd = [in_] + [x for x in (bias, scale) if isinstance(x, Tl)] + list(reads)
            tr.op('act', lambda e: e.activation(out=out.ap, in_=in_.ap, func=func, **kw), reads=rd, writes=[out])

        def tt(out, in0, in1, op, eng='dve'):
            tr.op(eng, lambda e: e.tensor_tensor(out=out.ap, in0=in0.ap, in1=in1.ap, op=op), reads=[in0, in1], writes=[out])

        def ts(out, in0, s1, s2, op0, op1):
            rd = [in0] + [x for x in (s1, s2) if isinstance(x, Tl)]
            a1_ = s1.ap if isinstance(s1, Tl) else s1
            a2_ = s2.ap if isinstance(s2, Tl) else s2
            tr.op('dve', lambda e: e.tensor_scalar(out=out.ap, in0=in0.ap, scalar1=a1_, scalar2=a2_, op0=op0, op1=op1),
                  reads=rd, writes=[out])

        def stt(out, in0, scalar, in1, op0, op1):
            rd = [in0, in1] + ([scalar] if isinstance(scalar, Tl) else [])
            sc = scalar.ap if isinstance(scalar, Tl) else scalar
            tr.op('dve', lambda e: e.scalar_tensor_tensor(out=out.ap, in0=in0.ap, scalar=sc, in1=in1.ap, op0=op0, op1=op1),
                  reads=rd, writes=[out])

        def dma(q, out, in_, sem, reads=(), writes=()):
            o_ap = out.ap if hasattr(out, 'keys') else out
            i_ap = in_.ap if hasattr(in_, 'keys') else in_
            tr.op(q, lambda e: e.dma_start(out=o_ap, in_=i_ap), reads=list(reads), writes=list(writes), dsem=sem)

        def wload(src, ncols):
            i = wslot[0] % NSLOT
            wslot[0] += 1
            t = Tl(WRb, "WR", i * 4096, ncols, BF16)
            full = Tl(WRb, "WR", i * 4096, 2048, BF16)
            dma('pool', t, src, wsem[i], writes=[full])
            return t

        def nextpd():
            p = pd[pdi[0] % 2]
            pdi[0] += 1
            return p

        fine_next = [False]

        def dense(src, acts, ncols=128):
            nk = len(acts)
            w = wload(src, nk * ncols)
            p = nextpd()
            fine = fine_next[0]
            fine_next[0] = False
            for half in range(2):
                c0, c1 = half * 512, (half + 1) * 512
                if fine:
                    for kc in range(nk):
                        tr.op('pe', lambda e, kc=kc, c0=c0, c1=c1: e.matmul(
                            p.ap[0:ncols, c0:c1], w.ap[:, kc * ncols:(kc + 1) * ncols], acts[kc].ap[:, c0:c1],
                            start=(kc == 0), stop=(kc == nk - 1)),
                            reads=[w, acts[kc].cols(c0, c1)], writes=[p.cols(c0, c1)])
                    continue
                fns = [(lambda e, kc=kc: e.matmul(p.ap[0:ncols, c0:c1], w.ap[:, kc * ncols:(kc + 1) * ncols],
                                                  acts[kc].ap[:, c0:c1], start=(kc == 0), stop=(kc == nk - 1)))
                       for kc in range(nk)]
                tr.op('pe', fns, reads=[w] + [a.cols(c0, c1) for a in acts], writes=[p.cols(c0, c1)])
            return p

        def ssq_norm(chunks, sqring, rs):
            p = nextpd()
            n = len(chunks)
            for i, xc in enumerate(chunks):
                sq = sqring[i % len(sqring)]
                act(sq, xc, AF.Square)
                for half in range(2):
                    c0, c1 = half * 512, (half + 1) * 512
                    tr.op('pe', lambda e, c0=c0, c1=c1, sq=sq, i=i: e.matmul(p.ap[:, c0:c1], ones.ap, sq.ap[:, c0:c1],
                                                                             start=(i == 0), stop=(i == n - 1)),
                          reads=[ones, sq.cols(c0, c1)], writes=[p.cols(c0, c1)])
            act(rs, p, AF.Ln, bias=EPS, scale=1.0 / D)
            act(rs, rs, AF.Exp, scale=-0.5)

        def ssq_accum(xc, i, n, sqring, banks):
            sq = sqring[i % 2]
            act(sq, xc, AF.Square)
            if i % 2 == 0:
                return
            sqp = sq_pair
            tt(sqp, sqring[0], sqring[1], ALU.add)
            for half in range(2):
                c0, c1 = half * 512, (half + 1) * 512
                bk = banks[half]
                tr.op('pe', lambda e, c0=c0, c1=c1, bk=bk: e.matmul(bk.ap[:, 0:512], ones.ap, sqp.ap[:, c0:c1],
                                                                   start=(i == 1), stop=(i == n - 1)),
                      reads=[ones, sqp.cols(c0, c1)], writes=[bk])

        def ssq_finish(rs, banks):
            for half in range(2):
                act(rs.cols(half * 512, half * 512 + 512), banks[half], AF.Ln, bias=EPS, scale=1.0 / D)
            act(rs, rs, AF.Exp, scale=-0.5)

        def debug_dump(nm, t):
            if nm in dbg_out:
                n = dbg_out[nm].shape[1]
                if t.dtype == BF16:
                    stg = r23(88 * KB, n, F32)
                    act(stg, t, AF.Copy)
                    t = stg
                dma('sp', dbg_out[nm][0:t.parts, 0:t.ncols], t, xsem[0], reads=[t])

        def chk(nm):
            if stop_after == nm:
                raise _Stop()

        try:
            dma('sp', cm, cst_d[:, 640:1664], xsem[0], writes=[cm])
            dma('pool', m2, cst_d[:, 128:640], csem[0], writes=[m2])
            dma('pool', ident, cst_d[:, 0:128], csem[1], writes=[ident])
            dma('sp', smalls, smalls_d[:, :], xsem[3], writes=[smalls])
            dma('sp', wgk2, wgk2_d[:, :], xsem[4], writes=[wgk2])
            dma('pool', wgkl, w_gkl_d[:, :], csem[2], writes=[wgkl])
            tr.op('dve', lambda e: e.memset(ones.ap, 1.0), writes=[ones])
            for t_ in persA + persB:
                tr.op('dve', lambda e, t_=t_: e.memset(t_.ap, 0.0), writes=[t_])
            act(condb, sm(SM_CT, 16), AF.Silu)
            tt(lbt, sm(SM_LB0, 8), sm(SM_LB1, 8), ALU.subtract)
            act(lb, lbt, AF.Sigmoid)
            act(omlb, lb, AF.Identity, bias=1.0, scale=-1.0)
            act(nbgk, sm(SM_BGK, 4), AF.Identity, scale=-1.0)

            sh1 = lambda fc: mod.cols(fc, fc + 1)
            gt1 = lambda fc: mod.cols(32 + fc, 33 + fc)
            sh2 = lambda fc: mod.cols(48 + fc, 49 + fc)
            gt2 = lambda fc: mod.cols(80 + fc, 81 + fc)
            sqringA = [r23(72 * KB, 1024, BF16), r23(74 * KB, 1024, BF16)]
            rsA = r23(76 * KB, 1024, F32)

            def phaseA_loads(s_):
                for fc in range(16):
                    dma('sp', xall[fc], xT[fc, :, s_ * ST:(s_ + 1) * ST], xsem[fc], writes=[xall[fc]])

            pdA = [None]

            sq_pairA = r23(78 * KB, 1024, BF16)

            def phaseA_ssq_chunk(i):
                if i == 0:
                    pdA[0] = nextpd()
                p = pdA[0]
                sq = sqringA[i % 2]
                act(sq, xall[i], AF.Square)
                if i % 2 == 0:
                    return
                tt(sq_pairA, sqringA[0], sqringA[1], ALU.add)
                for half in range(2):
                    c0, c1 = half * 512, (half + 1) * 512
                    tr.op('pe', lambda e, c0=c0, c1=c1, i=i: e.matmul(p.ap[:, c0:c1], ones.ap, sq_pairA.ap[:, c0:c1],
                                                                     start=(i == 1), stop=(i == 15)),
                          reads=[ones, sq_pairA.cols(c0, c1)], writes=[p.cols(c0, c1)])

            def phaseA_ssq_finish():
                p = pdA[0]
                act(rsA, p, AF.Ln, bias=EPS, scale=1.0 / D)
                act(rsA, rsA, AF.Exp, scale=-0.5)
                for fc in range(16):
                    tt(xall[fc], xall[fc], rsA, ALU.mult)

            def phaseA_ssq(s_):
                for i in range(16):
                    phaseA_ssq_chunk(i)
                phaseA_ssq_finish()

            def phaseA_scale(s_):
                for fc in range(16):
                    if fc % 2 == 0:
                        act(hT[fc], xall[fc], AF.Identity, bias=sh1(fc), scale=a1.cols(fc, fc + 1))
                    else:
                        ts(hT[fc], xall[fc], a1.cols(fc, fc + 1), sh1(fc), ALU.mult, ALU.add)

            phaseA_loads(0)

            class KeyT:
                def __init__(self, ap, keys):
                    self.ap, self.keys = ap, keys
            rowbuf = [cs(18 * KB, 512, F32, parts=1)]
            m128 = cs(20 * KB, 512, BF16)
            dma('pool', m128, cst_d[:, 1664:2176], csem[0], writes=[m128])
            modraw = cs(17 * KB + 512, 96, F32)
            scr_pm = scr_d.rearrange("(m p) -> p m", p=128)

            def ada_quarter(idx):
                blk, kq = idx // 4, idx % 4
                w = wload(w_ada_d[idx], 2048)
                fns = [(lambda e, j=j, w=w: e.matmul(pa.ap[0:1, 0:512], condb.ap[:, kq * 4 + j: kq * 4 + j + 1],
                                                     w.ap[:, j * 512:(j + 1) * 512],
                                                     start=(kq == 0 and j == 0), stop=(kq == 3 and j == 3)))
                       for j in range(4)]
                tr.op('pe', fns, reads=[w, condb], writes=[pa])
                if kq == 3:
                    rb = rowbuf[0]
                    tr.op('act', lambda e, rb=rb: e.activation(out=rb.ap, in_=pa.ap[0:1, 0:512], func=AF.Copy),
                          reads=[pa], writes=[rb])
                    seg = KeyT(scr_d[blk * 512:(blk + 1) * 512].rearrange("(o n) -> o n", o=1), (("SCR", blk),))
                    dma('sp', seg, rb, xsem[8 + blk % 8], reads=[rb], writes=[seg])

            def ada_readback(m0, m1):
                segs = [KeyT(None, (("SCR", blk),)) for blk in range(m0 // 4, m1 // 4)]
                dst = modraw.cols(m0, m1)
                tr.op('sp', lambda e: e.dma_start(out=dst.ap, in_=scr_pm[:, m0:m1], allow_slow_non_contiguous=True),
                      reads=segs, writes=[dst], dsem=xsem[6])
                tt(mod.cols(m0, m1), dst, sm(SM_BADA + m0, m1 - m0), ALU.add)

            for idx in range(32):
                ada_quarter(idx)
                if idx % 2 == 1:
                    phaseA_ssq_chunk(idx // 2)
            phaseA_ssq_finish()
            ada_readback(0, 32)
            stt(a1, mod.cols(16, 32), 1.0, sm(SM_N1, 16), ALU.add, ALU.mult)
            ada_next = [32]

            def ada_some(n):
                for _ in range(n):
                    idx = ada_next[0]
                    if idx >= 96:
                        return
                    ada_quarter(idx)
                    ada_next[0] = idx + 1
                    if idx == 95:
                        ada_readback(32, 96)
                        stt(a2, mod.cols(64, 80), 1.0, sm(SM_N2, 16), ALU.add, ALU.mult)
                        debug_dump('mod', mod)
            chk('p0')

            class BSet:
                pass

            def mkset(base):
                S = BSet()
                t = [r23(base + i * 2048, 1024, BF16) for i in range(10)]
                S.B0, S.B1, S.B2 = t[0], t[1], t[2]
                S.B3 = [t[3], t[4]]
                S.B4 = [t[5], t[6]]
                S.B5 = t[7]
                S.B6 = [t[8], t[9]]
                S.B7 = r23(base + 20 * KB, 512, BF16)
                S.B8 = [r23(base + 21 * KB, 512, BF16), r23(base + 22 * KB, 512, BF16)]
                return S
            bsets = [mkset(56 * KB), mkset(0)]
            for i_, S_ in enumerate(bsets):
                S_.ebl = r23(31 * KB + i_ * 64, 16, F32)
            o_sb = [r23(23 * KB, 512, F32), r23(25 * KB, 512, F32)]
            rstd_t = r23(27 * KB, 512, F32)
            t_t = r23(29 * KB, 512, F32)
            S_all = r23(79 * KB, 9 * 256, F32)
            S_bf = r23(88 * KB, 8 * 256, BF16)
            F0, F1, F2, F3, F4, F5 = Fw
            gkl = Tl(R23b, "R23", Wb + 5 * 4096, 1024, F32, parts=16)

            def transposes(src, dst):
                fns = [(lambda e, b=b: e.transpose(pt_bf[:, b * 128:(b + 1) * 128], src.ap[:, b * 128:(b + 1) * 128], ident.ap))
                       for b in range(8)]
                tr.op('pe', fns, reads=[src, ident], writes=[pt])
                tr.op('act', lambda e: e.activation(out=dst.ap, in_=pt_bf, func=AF.Copy), reads=[pt], writes=[dst])

            def bcast_last(t):
                return t.ap.rearrange("p (c t) -> p c t", t=64)[:, :, 63:64].broadcast_to([128, 16, 64])

            def v3(t):
                return t.ap.rearrange("p (c t) -> p c t", t=64)

            def H(t, hf):
                return t.cols(hf * 512, hf * 512 + 512)

            def v3h(t):
                return t.ap.rearrange("p (c t) -> p c t", t=64)

            def scan_h(dst, src):
                for hf in range(2):
                    d_, s_, c_ = H(dst, hf), H(src, hf), H(cm, hf)
                    tr.op('dve', lambda e, d_=d_, s_=s_, c_=c_: e.tensor_tensor_scan(
                        out=d_.ap, data0=c_.ap, data1=s_.ap, initial=0.0, op0=ALU.mult, op1=ALU.add),
                        reads=[c_, s_], writes=[d_])

            def kdec_and_ebl(S, C=64):
                nch = 512 // C
                for hf in range(2):
                    b1, b0, f3 = H(S.B1, hf), H(S.B0, hf), H(F3, hf)
                    vv = lambda t: t.ap.rearrange("p (c t) -> p c t", t=C)
                    tr.op('dve', lambda e, b1=b1, b0=b0, f3=f3: e.tensor_tensor(
                        out=vv(b1), in0=vv(b0), in1=vv(f3)[:, :, C - 1:C].broadcast_to([128, nch, C]), op=ALU.mult),
                        reads=[b0, f3], writes=[b1])
                    eb = S.ebl.cols(hf * nch, hf * nch + nch)
                    tr.op('dve', lambda e, eb=eb, f3=f3: e.tensor_copy(out=eb.ap.rearrange("p (c o) -> p c o", o=1),
                                                                       in_=vv(f3)[:, :, C - 1:C]),
                          reads=[f3], writes=[eb])

            def cumsum128_fix(t):
                for hf in range(2):
                    th = H(t, hf)
                    v4 = th.ap.rearrange("p (b two t) -> p b two t", two=2, t=64)
                    tr.op('dve', lambda e, v4=v4: e.tensor_tensor(out=v4[:, :, 1, :], in0=v4[:, :, 1, :],
                                                                 in1=v4[:, :, 0, 63:64].broadcast_to([128, 4, 64]), op=ALU.add),
                          reads=[th], writes=[th])

            def tail_transposes(S, ne):
                transposes(S.B1, S.B5)
                for ec in range(ne):
                    transposes(S.B3[ec], S.B6[ec])

            deferred = []

            def flush_deferred():
                while deferred:
                    deferred.pop(0)()

            def tail_gen(S, dv, pers, nw_col, ychunks, C=64):
                ne = dv // 128
                B0, B2, B5, B6, B7, B8 = S.B0, S.B2, S.B5, S.B6, S.B7, S.B8

                nch = 512 // C

                def stage_A(hf):
                    fns = [(lambda e, j=j, blk=4 * hf + j: e.matmul(pa.ap[:, j * 128:(j + 1) * 128],
                                                                    B0.ap[:, blk * 128:(blk + 1) * 128],
                                                                    B2.ap[:, blk * 128:(blk + 1) * 128], start=True, stop=True))
                           for j in range(4)]
                    tr.op('pe', fns, reads=[H(B0, hf), H(B2, hf)], writes=[pa])
                    tt(B7, pa, m2 if C == 64 else m128, ALU.mult)
                    s0 = S_all.cols(0, dv)
                    s_src = pers if hf == 0 else S_all.cols(nch * dv, (nch + 1) * dv)
                    tr.op('dve', lambda e, d_=s0, s_=s_src: e.tensor_copy(out=d_.ap, in_=s_.ap), reads=[s_src], writes=[s0])
                    spb = 512 // dv
                    per_round = 2 * spb
                    banks = [ps, po]
                    nr = nch // per_round
                    for r in range(nr):
                        fns = []
                        rd = []
                        loc = []
                        for k in range(per_round):
                            ci = r * per_round + k
                            c = nch * hf + ci
                            if C == 64:
                                blk, p0, kp = c // 2, 64 * (c % 2), 64
                                bk, sl = banks[c % 2], k // 2
                            else:
                                blk, p0, kp = c, 0, 128
                                bk, sl = banks[k // spb], k % spb
                            loc.append((ci, c, bk, sl))
                            for ec in range(ne):
                                fns.append(lambda e, sl=sl, ec=ec, blk=blk, p0=p0, kp=kp, bk=bk: e.matmul(
                                    bk.ap[:, sl * dv + ec * 128: sl * dv + ec * 128 + 128],
                                    B5.ap[p0:p0 + kp, blk * 128:(blk + 1) * 128],
                                    B6[ec].ap[p0:p0 + kp, blk * 128:(blk + 1) * 128], start=True, stop=True))
                                rd += [B5.cols(blk * 128, blk * 128 + 128), B6[ec].cols(blk * 128, blk * 128 + 128)]
                        tr.op('pe', fns, reads=rd, writes=[ps, po])
                        for (ci, c, bk, sl) in loc:
                            stt(S_all.cols((ci + 1) * dv, (ci + 2) * dv), S_all.cols(ci * dv, (ci + 1) * dv),
                                S.ebl.cols(c, c + 1), bk.cols(sl * dv, (sl + 1) * dv), ALU.mult, ALU.add)
                            if ci == nch // 2 - 1:
                                hh = nch // 2
                                tr.op('dve', lambda e, d_=S_bf.cols(0, hh * dv), s_=S_all.cols(0, hh * dv): e.tensor_copy(out=d_.ap, in_=s_.ap),
                                      reads=[S_all.cols(0, hh * dv)], writes=[S_bf.cols(0, hh * dv)])
                        if r + 1 < nr:
                            yield
                    hh = nch // 2
                    tr.op('dve', lambda e, d_=S_bf.cols(hh * dv, nch * dv), s_=S_all.cols(hh * dv, nch * dv): e.tensor_copy(out=d_.ap, in_=s_.ap),
                          reads=[S_all.cols(hh * dv, nch * dv)], writes=[S_bf.cols(hh * dv, nch * dv)])
                    if hf == 1:
                        tr.op('dve', lambda e, d_=pers, s_=S_all.cols(nch * dv, (nch + 1) * dv): e.tensor_copy(out=d_.ap, in_=s_.ap),
                              reads=[S_all.cols(nch * dv, (nch + 1) * dv)], writes=[pers])

                def stage_O(hf):
                    hh = nch // 2
                    for ec in range(ne):
                        for jh in range(2):
                            fns = []
                            for j in range(2 * jh, 2 * jh + 2):
                                blk = 4 * hf + j
                                fns.append(lambda e, j=j, blk=blk, ec=ec: e.matmul(
                                    pa.ap[:, j * 128:(j + 1) * 128], B6[ec].ap[:, blk * 128:(blk + 1) * 128],
                                    B7.ap[:, j * 128:(j + 1) * 128], start=True, stop=False))
                                npc = 128 // C
                                for cc in range(npc):
                                    ci = j * npc + cc
                                    c0 = blk * 128 + cc * C
                                    fns.append(lambda e, j=j, cc=cc, ci=ci, c0=c0, ec=ec, npc=npc: e.matmul(
                                        pa.ap[:, j * 128 + cc * C: j * 128 + cc * C + C],
                                        S_bf.ap[:, ci * dv + ec * 128: ci * dv + ec * 128 + 128],
                                        B2.ap[:, c0:c0 + C], start=False, stop=(cc == npc - 1)))
                            tr.op('pe', fns, reads=[H(B6[ec], hf), B7.cols(jh * 256, jh * 256 + 256),
                                                    S_bf.cols(jh * hh * dv, (jh + 1) * hh * dv), H(B2, hf)],
                                  writes=[pa.cols(jh * 256, jh * 256 + 256)])
                        tr.op('dve', lambda e, d_=o_sb[ec]: e.tensor_copy(out=d_.ap, in_=pa.ap), reads=[pa], writes=[o_sb[ec]])
                        act(B8[ec], o_sb[ec], AF.Square)

                def stage_N(hf):
                    nb = pt if hf == 0 else ps
                    fns = [(lambda e, ec=ec: e.matmul(nb.ap[:, 0:512], ones.ap, B8[ec].ap, start=(ec == 0), stop=(ec == ne - 1)))
                           for ec in range(ne)]
                    tr.op('pe', fns, reads=[ones] + B8[:ne], writes=[nb])
                    act(rstd_t, nb, AF.Ln, bias=EPS, scale=1.0 / dv)
                    act(rstd_t, rstd_t, AF.Exp, scale=-0.5)

                    def fin():
                        for ec in range(ne):
                            tt(t_t, o_sb[ec], rstd_t, ALU.mult)
                            stt(H(ychunks[ec], hf), t_t, sm(nw_col + ec), H(S.B4[ec], hf), ALU.mult, ALU.mult)
                    return fin

                for _ in stage_A(0):
                    yield
                flush_deferred()
                yield
                stage_O(0)
                yield
                fin0 = stage_N(0)
                for _ in stage_A(1):
                    yield
                fin0()
                yield
                stage_O(1)
                yield
                deferred.append(stage_N(1))
                yield

            def frontA_gen(h, S):
                p = dense(w_in_d[8 + h], hT)
                for hf in range(2):
                    act(H(F0, hf), H(p, hf), AF.Sigmoid)
                for hf in range(2):
                    ts(H(F0, hf), H(F0, hf), omlb.cols(h, h + 1), lb.cols(h, h + 1), ALU.mult, ALU.add)
                for hf in range(2):
                    ts(H(F1, hf), H(F0, hf), -1.0, 1.0, ALU.mult, ALU.add)
                for hf in range(2):
                    act(H(F0, hf), H(F0, hf), AF.Ln)
                yield
                scan_h(F2, F0)
                for hf in range(2):
                    act(H(F3, hf), H(F2, hf), AF.Exp)
                for hf in range(2):
                    act(H(F4, hf), H(F2, hf), AF.Exp, scale=-1.0)
                for hf in range(2):
                    tt(H(S.B0, hf), H(F1, hf), H(F4, hf), ALU.mult)
                kdec_and_ebl(S)
                p = dense(w_in_d[0 + h], hT)
                for hf in range(2):
                    act(H(F5, hf), H(p, hf), AF.Silu)
                for hf in range(2):
                    tt(H(S.B2, hf), H(F5, hf), H(F3, hf), ALU.mult)
                yield
                p = dense(w_in_d[16 + h], hT)
                for hf in range(2):
                    tr.op('dve', lambda e, d_=H(S.B3[0], hf), p_=H(p, hf): e.tensor_copy(out=d_.ap, in_=p_.ap), reads=[H(p, hf)], writes=[H(S.B3[0], hf)])
                yield
                transposes(S.B1, S.B5)
                p = dense(w_in_d[24 + h], hT)
                transposes(S.B3[0], S.B6[0])
                yield
                for hf in range(2):
                    act(H(S.B4[0], hf), H(p, hf), AF.Silu)
                yield

            def gk_low():
                p = nextpd()
                for half in range(2):
                    c0, c1 = half * 512, (half + 1) * 512
                    fns = [(lambda e, kc=kc, c0=c0, c1=c1: e.matmul(p.ap[0:16, c0:c1], wgkl.ap[:, kc * 16:(kc + 1) * 16],
                                                                    hT[kc].ap[:, c0:c1], start=(kc == 0), stop=(kc == 15)))
                           for kc in range(16)]
                    tr.op('pe', fns, reads=[wgkl] + [a.cols(c0, c1) for a in hT], writes=[p.cols(c0, c1)])
                tr.op('act', lambda e: e.activation(out=gkl.ap, in_=p.ap[0:16, :], func=AF.Copy), reads=[p], writes=[gkl])

            def frontB_gen(h, S):
                if h == 0:
                    gk_low()
                p = nextpd()
                for half in range(2):
                    c0, c1 = half * 512, (half + 1) * 512
                    tr.op('pe', lambda e, c0=c0, c1=c1: e.matmul(p.ap[:, c0:c1], wgk2.ap[0:16, h * 128:(h + 1) * 128],
                                                                 gkl.ap[0:16, c0:c1], start=True, stop=True),
                          reads=[wgk2, gkl], writes=[p.cols(c0, c1)])
                for hf in range(2):
                    act(H(F0, hf), H(p, hf), AF.Exp, bias=nbgk.cols(h, h + 1), scale=-1.0)
                for hf in range(2):
                    act(H(F0, hf), H(F0, hf), AF.Ln, bias=1.0)
                yield
                scan_h(F2, F0)
                cumsum128_fix(F2)
                for hf in range(2):
                    act(H(F3, hf), H(F2, hf), AF.Exp, scale=-1.0 / 16.0)
                for hf in range(2):
                    act(H(F4, hf), H(F2, hf), AF.Exp, scale=1.0 / 16.0)
                p = dense(w_in_d[36 + h], hT)
                for hf in range(2):
                    tt(H(S.B0, hf), H(p, hf), H(F4, hf), ALU.mult)
                kdec_and_ebl(S, C=128)
                yield
                p = dense(w_in_d[32 + h], hT)
                for hf in range(2):
                    stt(H(S.B2, hf), H(p, hf), 128.0 ** -0.5, H(F3, hf), ALU.mult, ALU.mult)
                yield
                p = dense(w_in_d[40 + 2 * h], hT)
                for hf in range(2):
                    tr.op('dve', lambda e, d_=H(S.B3[0], hf), p_=H(p, hf): e.tensor_copy(out=d_.ap, in_=p_.ap), reads=[H(p, hf)], writes=[H(S.B3[0], hf)])
                yield
                p = dense(w_in_d[41 + 2 * h], hT)
                for hf in range(2):
                    tr.op('dve', lambda e, d_=H(S.B3[1], hf), p_=H(p, hf): e.tensor_copy(out=d_.ap, in_=p_.ap), reads=[H(p, hf)], writes=[H(S.B3[1], hf)])
                yield
                p = dense(w_in_d[48 + 2 * h], hT)
                for hf in range(2):
                    act(H(S.B4[0], hf), H(p, hf), AF.Silu)
                yield
                transposes(S.B1, S.B5)
                p = dense(w_in_d[49 + 2 * h], hT)
                transposes(S.B3[0], S.B6[0])
                transposes(S.B3[1], S.B6[1])
                yield
                for hf in range(2):
                    act(H(S.B4[1], hf), H(p, hf), AF.Silu)
                yield

            def step(g):
                if g is None:
                    return False
                try:
                    next(g)
                    return True
                except StopIteration:
                    return False

            for s in range(NST):
                tok = slice(s * ST, (s + 1) * ST)
                sqring = [r23(64 * KB, 1024, BF16), r23(66 * KB, 1024, BF16)]
                rs = r23(68 * KB, 1024, F32)
                nbanks = [pt, ps]
                phaseA_scale(s)
                fine_next[0] = True
                if s == 0:
                    debug_dump("hT0", hT[0])
                    chk('A')

                heads = []
                for h in range(8):
                    heads.append(('A', h))
                for h in range(4):
                    heads.append(('B', h))
                tg = None
                for i, (kind, h) in enumerate(heads):
                    S = bsets[(i + 1) % 2]
                    if kind == 'A':
                        fg = frontA_gen(h, S)
                        mk_tail = (lambda S=S, h=h: tail_gen(S, 128, persA[h], SM_ANW, [yT[h]]))
                    else:
                        fg = frontB_gen(h, S)
                        mk_tail = (lambda S=S, h=h: tail_gen(S, 256, persB[h], SM_BNW, [yT[8 + 2 * h], yT[9 + 2 * h]], C=128))
                    while True:
                        a_ = step(tg)
                        b_ = step(fg)
                        if not a_ and not b_:
                            break
                    tg = mk_tail()
                def phaseC_gen(fc):
                    p = dense(w_in_d[56 + fc], hT)
                    act(F0, p, AF.Sigmoid)
                    if s == 0 and fc > 0:
                        ada_some(2 if fc == 1 else 1)
                    yield
                    p = dense(w_upa_d[fc], yT[0:8])
                    tt(F1, p, F0, ALU.mult)
                    if s == 0 and fc > 0:
                        ada_some(2 if fc == 1 else 1)
                    yield
                    p = dense(w_in_d[72 + fc], hT)
                    act(F2, p, AF.Sigmoid)
                    if s == 0 and fc > 0:
                        ada_some(2 if fc == 1 else 1)
                    yield
                    p = dense(w_upb_d[fc], yT[8:16])
                    tt(F3, p, F2, ALU.mult)
                    tt(mg[fc], F1, F3, ALU.add)
                    if s == 0 and fc > 0:
                        ada_some(2 if fc == 1 else 1)
                    yield
                for fc in range(16):
                    cg = phaseC_gen(fc)
                    if fc == 0:
                        step(tg); step(cg)
                        step(tg); step(cg)
                        while step(tg):
                            pass
                        flush_deferred()
                    while step(cg):
                        pass
                if s == 0:
                    debug_dump("yA0", yT[0])
                    debug_dump("yB0", yT[8])
                if s == 0:
                    debug_dump("mg0", mg[0])
                    chk('C')

                for fc in range(16):
                    dma('sp', xres[fc], xT[fc, :, tok], xsem[fc], writes=[xres[fc]])
                    p = dense(w_o_d[fc], mg)
                    stt(xres[fc], p, gt1(fc), xres[fc], ALU.mult, ALU.add)
                    if fc > 0:
                        ssq_accum(xres[fc - 1], fc - 1, 16, sqring, nbanks)
                ssq_accum(xres[15], 15, 16, sqring, nbanks)
                if s == 0:
                    debug_dump("x1_0", xres[0])
                    chk('D')
                ssq_finish(rs, nbanks)
                tmpr = [r23(72 * KB, 1024, F32), r23(76 * KB, 1024, F32)]
                for fc in range(16):
                    tm = tmpr[fc % 2]
                    tt(tm, xres[fc], rs, ALU.mult)
                    act(mg[fc], tm, AF.Identity, bias=sh2(fc), scale=a2.cols(fc, fc + 1))
                fine_next[0] = True

                actT = [r23(Wb + j * 2048, 1024, BF16) for j in range(11)]
                sgr = [r23(56 * KB, 1024, F32), r23(60 * KB, 1024, F32)]
                for q in range(4):
                    for j in range(11):
                        hc = q * 11 + j
                        sg = sgr[j % 2]
                        p = dense(w_fi_d[hc], mg)
                        act(sg, p, AF.Silu)
                        p = dense(w_fi_d[44 + hc], mg)
                        tt(actT[j], p, sg, ALU.mult)
                    for fc in range(16):
                        p = dense(w_fo_d[fc, :, q * 1408:(q + 1) * 1408], actT)
                        stt(xres[fc], p, gt2(fc), xres[fc], ALU.mult, ALU.add)
                        if q == 3 and fc > 0:
                            ssq_accum(xres[fc - 1], fc - 1, 16, sqring, nbanks)
                ssq_accum(xres[15], 15, 16, sqring, nbanks)

                if s + 1 < NST:
                    phaseA_loads(s + 1)
                ssq_finish(rs, nbanks)
                for fc in range(16):
                    stt(xres[fc], xres[fc], sm(SM_NF + fc), rs, ALU.mult, ALU.mult)
                    dma('sp', outT[fc, :, tok], xres[fc], osem[fc], reads=[xres[fc]])
                if s + 1 < NST:
                    phaseA_ssq(s + 1)


        except _Stop:
            pass
        for sx in xsem + osem:
            if sx.count > 0:
                tr._wait('sp', sx, sx.count)
        print("instructions:", tr.ninst, "sems:", tr.nsem, "pe:", tr.cnt['pe'], "act:", tr.cnt['act'], "dve:", tr.cnt['dve'])
    return nc


def _colchunks(W):
    K, N = W.shape
    return np.ascontiguousarray(W.reshape(K // 128, 128, N // 128, 128).transpose(2, 1, 0, 3)).reshape(N // 128, 128, (K // 128) * 128)


def _pcol(v):
    return np.ascontiguousarray(v.reshape(-1, 128).T)


def make_consts():
    ident = np.eye(128, dtype=np.float32)
    j = np.arange(128)[:, None]
    i = np.arange(128)[None, :]
    m2 = ((j // 64 == i // 64) & (j <= i)).astype(np.float32)
    m2t = np.tile(m2, (1, 4))
    cmv = np.ones((128, 1024), np.float32)
    cmv[:, ::64] = 0.0
    m128t = np.tile((j <= i).astype(np.float32), (1, 4))
    return np.concatenate([ident, m2t, cmv, m128t], axis=1)


def prep_inputs(x, c, w_ada, b_ada, norm1_w, w_in, w_gk2, b_gk2, lb_param, a_norm_w, b_norm_w,
                w_up_a, w_up_b, w_o, norm2_w, w_ffn_in, w_ffn_out, final_norm_w):
    f = lambda a: np.asarray(a, dtype=np.float32)
    x, c = f(x), f(c)
    w_in0 = f(w_in)[0]
    w_in_ng = np.concatenate([w_in0[:, :7168], w_in0[:, 7184:]], axis=1)
    shared = {
        "wgk2": np.ascontiguousarray(f(w_gk2)[0]),
        "cst": make_consts(),
        "w_ada_l": np.ascontiguousarray(f(w_ada)[0].reshape(4, 4, 128, 24, 512).transpose(3, 0, 2, 1, 4)).reshape(96, 128, 2048),
        "w_in_l": _colchunks(w_in_ng),
        "w_gkl": np.ascontiguousarray(w_in0[:, 7168:7184].reshape(16, 128, 16).transpose(1, 0, 2)).reshape(128, 256),
        "w_upa_l": _colchunks(f(w_up_a)[0]),
        "w_upb_l": _colchunks(f(w_up_b)[0]),
        "w_o_l": _colchunks(f(w_o)[0]),
        "w_fi_l": _colchunks(f(w_ffn_in)[0]),
        "w_fo_l": _colchunks(f(w_ffn_out)[0]),
    }
    sm_common = np.zeros((128, NSM), np.float32)
    sm_common[:, SM_BADA:SM_BADA + 96] = _pcol(f(b_ada)[0])
    sm_common[:, SM_N1:SM_N1 + 16] = _pcol(f(norm1_w)[0])
    sm_common[:, SM_N2:SM_N2 + 16] = _pcol(f(norm2_w)[0])
    sm_common[:, SM_NF:SM_NF + 16] = _pcol(f(final_norm_w))
    sm_common[:, SM_LB0:SM_LB0 + 8] = _pcol(f(lb_param)[0])
    sm_common[:, SM_LB1:SM_LB1 + 8] = _pcol(f(lb_param)[1])
    sm_common[:, SM_ANW:SM_ANW + 1] = _pcol(f(a_norm_w)[0])
    sm_common[:, SM_BNW:SM_BNW + 2] = _pcol(f(b_norm_w)[0])
    sm_common[:, SM_BGK:SM_BGK + 4] = _pcol(f(b_gk2)[0])
    in_maps = []
    for b in range(8):
        smb = sm_common.copy()
        smb[:, SM_CT:SM_CT + 16] = _pcol(c[b])
        m = dict(shared)
        m["smalls"] = smb
        m["xT"] = np.ascontiguousarray(x[b].T).reshape(16, 128, T)
        in_maps.append(m)
    return in_maps


_NC_CACHE = {}


def kernel(**inputs):
    in_maps = prep_inputs(**inputs)
    if "nc" not in _NC_CACHE:
        _NC_CACHE["nc"] = build_nc()
    nc = _NC_CACHE["nc"]
    res = run_bass_kernel_spmd(nc, in_maps, core_ids=list(range(8)))
    out = np.empty((8, T, D), np.float32)
    for b in range(8):
        out[b] = res.results[b]["outT"].reshape(D, T).T
    return out
```

```python
import numpy as np
import concourse.bass as bass
import concourse.mybir as mybir
from concourse.bass_utils import run_bass_kernel_spmd
from contextlib import ExitStack

F32 = mybir.dt.float32
BF16 = mybir.dt.bfloat16
AF = mybir.ActivationFunctionType
ALU = mybir.AluOpType

D = 2048
T = 2048
ST = 1024
NST = T // ST
KC = 16
EPS = 1e-6
UNIT = 512
EPOCH = 30000
NSLOT = 7

SM_BADA = 0
SM_N1 = 96
SM_N2 = 112
SM_NF = 128
SM_LB0 = 144
SM_LB1 = 152
SM_ANW = 160
SM_BNW = 161
SM_BGK = 163
SM_CT = 167
NSM = 183


class Sem:
    def __init__(self, h):
        self.h = h
        self.count = 0


class Tl:
    def __init__(self, base, name, off, ncols, dtype, parts=128, p0=0):
        self.base = base
        self.name = name
        self.off = off
        self.ncols = ncols
        self.dtype = dtype
        self.parts = parts
        self.p0 = p0
        self.esz = 2 if dtype == BF16 else 4
        size = ncols * self.esz
        assert off % 4 == 0 and size % 4 == 0, (off, size)
        ap = base[p0:p0 + parts, off // 4:(off + size) // 4]
        if dtype == BF16:
            ap = ap.bitcast(BF16)
        self.ap = ap
        self.keys = tuple((name, u) for u in range(off // UNIT, (off + size - 1) // UNIT + 1))

    def cols(self, lo, hi, p0=None, parts=None):
        p0 = self.p0 if p0 is None else p0
        parts = self.parts if parts is None else parts
        return Tl(self.base, self.name, self.off + lo * self.esz, hi - lo, self.dtype, parts, p0)


class Tracker:
    def __init__(self, nc, es):
        self.nc = nc
        self.es = es
        self.eng = {'pe': nc.tensor, 'act': nc.scalar, 'dve': nc.vector, 'pool': nc.gpsimd, 'sp': nc.sync}
        self.cnt = {e: 0 for e in self.eng}
        self.esems = {e: [] for e in self.eng}
        self.known = {e: {} for e in self.eng}
        self.lastw = {}
        self.readers = {}
        self.nsem = 0
        self.ninst = 0

    def newsem(self, name):
        self.nsem += 1
        return Sem(self.es.enter_context(self.nc.semaphore(f"{name}_{self.nsem}")))

    def _cursem(self, e):
        ep = self.cnt[e] // EPOCH
        while len(self.esems[e]) <= ep:
            self.esems[e].append(self.newsem(f"e_{e}"))
        return self.esems[e][ep]

    def _wait(self, e, sem, val):
        if self.known[e].get(sem, 0) < val:
            self.eng[e].wait_ge(sem.h, val)
            self.known[e][sem] = val
            self.ninst += 1

    def op(self, e, fns, reads=(), writes=(), dsem=None):
        if not isinstance(fns, (list, tuple)):
            fns = [fns]
        deps = {}

        def add(ev):
            s, v = ev
            if deps.get(s, 0) < v:
                deps[s] = v
        for t in reads:
            for k in t.keys:
                w = self.lastw.get(k)
                if w:
                    add(w)
        for t in writes:
            for k in t.keys:
                w = self.lastw.get(k)
                if w:
                    add(w)
                for ev in self.readers.get(k, {}).items():
                    add(ev)
        own = self._cursem(e) if dsem is None else None
        for s, v in deps.items():
            if e == 'pe' and s in self.esems['pe']:
                continue
            self._wait(e, s, v)
        if dsem is not None and dsem.count > 0:
            self._wait(e, dsem, dsem.count)
        ins = None
        for fn in fns:
            ins = fn(self.eng[e])
            self.ninst += 1
        if dsem is None:
            self.cnt[e] += 1
            local = (self.cnt[e] - 1) % EPOCH + 1
            ins.then_inc(own.h, 1)
            ev = (own, local)
        else:
            dsem.count += 16
            ins.then_inc(dsem.h, 16)
            ev = (dsem, dsem.count)
        for t in reads:
            for k in t.keys:
                r = self.readers.setdefault(k, {})
                if r.get(ev[0], 0) < ev[1]:
                    r[ev[0]] = ev[1]
        for t in writes:
            for k in t.keys:
                self.lastw[k] = ev
                self.readers[k] = {}
        return ev

    def finish(self):
        pass


class _Stop(Exception):
    pass


def build_nc(dbg=None, stop_after=None):
    nc = bass.Bass("TRN2", target_bir_lowering=False)

    def din(name, shape):
        return nc.dram_tensor(name, list(shape), F32, kind="ExternalInput").ap()
    xT = din("xT", [16, 128, T])
    smalls_d = din("smalls", [128, NSM])
    wgk2_d = din("wgk2", [16, 512])
    cst_d = din("cst", [128, 128 + 512 + 1024 + 512])
    w_ada_d = din("w_ada_l", [96, 128, 2048])
    w_in_d = din("w_in_l", [88, 128, 2048])
    w_gkl_d = din("w_gkl", [128, 256])
    w_upa_d = din("w_upa_l", [16, 128, 1024])
    w_upb_d = din("w_upb_l", [16, 128, 1024])
    w_o_d = din("w_o_l", [16, 128, 2048])
    w_fi_d = din("w_fi_l", [88, 128, 2048])
    w_fo_d = din("w_fo_l", [16, 128, 5632])
    outT = nc.dram_tensor("outT", [16, 128, T], F32, kind="ExternalOutput").ap()
    scr_d = nc.dram_tensor("scr", [96 * 128], F32, kind="Internal").ap()
    dbg_out = {}
    if dbg:
        for nm, ncols in dbg.items():
            dbg_out[nm] = nc.dram_tensor("dbg_" + nm, [128, ncols], F32, kind="ExternalOutput").ap()

    with ExitStack() as es:
        tr = Tracker(nc, es)
        KB = 1024
        R1b = es.enter_context(nc.sbuf_tensor("R1", [128, 64 * KB // 4], F32))
        R23b = es.enter_context(nc.sbuf_tensor("R23", [128, 92 * KB // 4], F32))
        WRb = es.enter_context(nc.sbuf_tensor("WR", [128, NSLOT * 4 * KB // 4], F32))
        CSb = es.enter_context(nc.sbuf_tensor("CS", [128, 22 * KB // 4], F32))
        PD = [es.enter_context(nc.psum_tensor(f"PD{i}", [128, 1024], F32)) for i in range(2)]
        PTb = es.enter_context(nc.psum_tensor("PT", [128, 512], F32))
        PAb = es.enter_context(nc.psum_tensor("PA", [128, 512], F32))
        PSb = es.enter_context(nc.psum_tensor("PS", [128, 512], F32))
        POb = es.enter_context(nc.psum_tensor("PO", [128, 512], F32))

        class PsT(Tl):
            def __init__(self, base, name, off, ncols, dtype, parts=128, p0=0):
                self.base, self.name, self.off, self.ncols, self.dtype = base, name, off, ncols, dtype
                self.parts, self.p0 = parts, p0
                self.esz = 2 if dtype == BF16 else 4
                lo = off // self.esz
                self.ap = base[p0:p0 + parts, lo:lo + ncols]
                size = ncols * self.esz
                self.keys = tuple((name, u) for u in range(off // UNIT, (off + size - 1) // UNIT + 1))

            def cols(self, lo, hi, p0=None, parts=None):
                p0 = self.p0 if p0 is None else p0
                parts = self.parts if parts is None else parts
                return PsT(self.base, self.name, self.off + lo * self.esz, hi - lo, self.dtype, parts, p0)

        pd = [PsT(PD[i], f"PD{i}", 0, 1024, F32) for i in range(2)]
        pt = PsT(PTb, "PT", 0, 512, F32)
        pt_bf = PTb[:, :].bitcast(BF16)
        po_bf = POb[:, :].bitcast(BF16)
        ps_bf = PSb[:, :].bitcast(BF16)
        pa = PsT(PAb, "PA", 0, 512, F32)
        ps = PsT(PSb, "PS", 0, 512, F32)
        po = PsT(POb, "PO", 0, 512, F32)

        def r1(off, n, dt):
            return Tl(R1b, "R1", off, n, dt)

        def r23(off, n, dt, parts=128):
            return Tl(R23b, "R23", off, n, dt, parts)

        def cs(off, n, dt, parts=128):
            return Tl(CSb, "CS", off, n, dt, parts)

        o = 0
        cm = cs(o, 1024, F32); o += 4096
        m2 = cs(o, 512, BF16); o += 1024
        ident = cs(o, 128, BF16); o += 256
        ones = cs(o, 128, BF16); o += 256
        smalls = cs(o, NSM, F32); o += 768
        mod = cs(o, 96, F32); o += 384
        a1 = cs(o, 16, F32); o += 64
        a2 = cs(o, 16, F32); o += 64
        lb = cs(o, 8, F32); o += 32
        omlb = cs(o, 8, F32); o += 32
        nbgk = cs(o, 4, F32); o += 16
        condb = cs(o, 16, BF16); o += 32
        lbt = cs(o, 8, F32); o += 32
        wgk2 = cs(o, 512, F32, parts=16); o += 2048
        wgkl = cs(o, 256, BF16); o += 512
        persA = [cs(o + h * 512, 128, F32) for h in range(8)]; o += 4096
        persB = [cs(o + h * 1024, 256, F32) for h in range(4)]; o += 4096
        assert o <= 17 * KB + 512

        def sm(c0, n=1):
            return smalls.cols(c0, c0 + n)

        hT = [r1(kc * 2048, 1024, BF16) for kc in range(16)]
        yT = [r1(32 * KB + c * 2048, 1024, BF16) for c in range(16)]
        xres = [r1(fc * 4096, 1024, F32) for fc in range(16)]
        mg = [r23(c * 2048, 1024, BF16) for c in range(16)]
        xall = [r23(fc * 4096, 1024, F32) for fc in range(16)]
        Wb = 32 * KB
        Fw = [r23(Wb + i * 4096, 1024, F32) for i in range(6)]

        xsem = [tr.newsem("x") for _ in range(16)]
        wsem = [tr.newsem("w") for _ in range(NSLOT)]
        csem = [tr.newsem("c") for _ in range(3)]
        osem = [tr.newsem("o") for _ in range(16)]
        wslot = [0]
        pdi = [0]

        def act(out, in_, func, bias=None, scale=None, reads=(), extra_w=()):
            kw = {}
            if bias is not None:
                kw['bias'] = bias.ap if isinstance(bias, Tl) else bias
            if scale is not None:
                kw['scale'] = scale.ap if isinstance(scale, Tl) else scale
            rd = [in_] + [x for x in (bias, scale) if isinstance(x, Tl)] + list(reads)
            tr.op('act', lambda e: e.activation(out=out.ap, in_=in_.ap, func=func, **kw), reads=rd, writes=[out])

        def tt(out, in0, in1, op, eng='dve'):
            tr.op(eng, lambda e: e.tensor_tensor(out=out.ap, in0=in0.ap, in1=in1.ap, op=op), reads=[in0, in1], writes=[out])

        def ts(out, in0, s1, s2, op0, op1):
            rd = [in0] + [x for x in (s1, s2) if isinstance(x, Tl)]
            a1_ = s1.ap if isinstance(s1, Tl) else s1
            a2_ = s2.ap if isinstance(s2, Tl) else s2
            tr.op('dve', lambda e: e.tensor_scalar(out=out.ap, in0=in0.ap, scalar1=a1_, scalar2=a2_, op0=op0, op1=op1),
                  reads=rd, writes=[out])

        def stt(out, in0, scalar, in1, op0, op1):
            rd = [in0, in1] + ([scalar] if isinstance(scalar, Tl) else [])
            sc = scalar.ap if isinstance(scalar, Tl) else scalar
            tr.op('dve', lambda e: e.scalar_tensor_tensor(out=out.ap, in0=in0.ap, scalar=sc, in1=in1.ap, op0=op0, op1=op1),
                  reads=rd, writes=[out])

        def dma(q, out, in_, sem, reads=(), writes=()):
            o_ap = out.ap if hasattr(out, 'keys') else out
            i_ap = in_.ap if hasattr(in_, 'keys') else in_
            tr.op(q, lambda e: e.dma_start(out=o_ap, in_=i_ap), reads=list(reads), writes=list(writes), dsem=sem)

        def wload(src, ncols):
            i = wslot[0] % NSLOT
            wslot[0] += 1
            t = Tl(WRb, "WR", i * 4096, ncols, BF16)
            full = Tl(WRb, "WR", i * 4096, 2048, BF16)
            dma('pool', t, src, wsem[i], writes=[full])
            return t

        def nextpd():
            p = pd[pdi[0] % 2]
            pdi[0] += 1
            return p

        fine_next = [False]

        def dense(src, acts, ncols=128):
            nk = len(acts)
            w = wload(src, nk * ncols)
            p = nextpd()
            fine = fine_next[0]
            fine_next[0] = False
            for half in range(2):
                c0, c1 = half * 512, (half + 1) * 512
                if fine:
                    for kc in range(nk):
                        tr.op('pe', lambda e, kc=kc, c0=c0, c1=c1: e.matmul(
                            p.ap[0:ncols, c0:c1], w.ap[:, kc * ncols:(kc + 1) * ncols], acts[kc].ap[:, c0:c1],
                            start=(kc == 0), stop=(kc == nk - 1)),
                            reads=[w, acts[kc].cols(c0, c1)], writes=[p.cols(c0, c1)])
                    continue
                fns = [(lambda e, kc=kc: e.matmul(p.ap[0:ncols, c0:c1], w.ap[:, kc * ncols:(kc + 1) * ncols],
                                                  acts[kc].ap[:, c0:c1], start=(kc == 0), stop=(kc == nk - 1)))
                       for kc in range(nk)]
                tr.op('pe', fns, reads=[w] + [a.cols(c0, c1) for a in acts], writes=[p.cols(c0, c1)])
            return p

        def ssq_norm(chunks, sqring, rs):
            p = nextpd()
            n = len(chunks)
            for i, xc in enumerate(chunks):
                sq = sqring[i % len(sqring)]
                act(sq, xc, AF.Square)
                for half in range(2):
                    c0, c1 = half * 512, (half + 1) * 512
                    tr.op('pe', lambda e, c0=c0, c1=c1, sq=sq, i=i: e.matmul(p.ap[:, c0:c1], ones.ap, sq.ap[:, c0:c1],
                                                                             start=(i == 0), stop=(i == n - 1)),
                          reads=[ones, sq.cols(c0, c1)], writes=[p.cols(c0, c1)])
            act(rs, p, AF.Ln, bias=EPS, scale=1.0 / D)
            act(rs, rs, AF.Exp, scale=-0.5)

        def ssq_accum(xc, i, n, sqring, banks):
            sq = sqring[i % len(sqring)]
            act(sq, xc, AF.Square)
            for half in range(2):
                c0, c1 = half * 512, (half + 1) * 512
                bk = banks[half]
                tr.op('pe', lambda e, c0=c0, c1=c1, sq=sq, bk=bk: e.matmul(bk.ap[:, 0:512], ones.ap, sq.ap[:, c0:c1],
                                                                         start=(i == 0), stop=(i == n - 1)),
                      reads=[ones, sq.cols(c0, c1)], writes=[bk])

        def ssq_finish(rs, banks):
            for half in range(2):
                act(rs.cols(half * 512, half * 512 + 512), banks[half], AF.Ln, bias=EPS, scale=1.0 / D)
            act(rs, rs, AF.Exp, scale=-0.5)

        def debug_dump(nm, t):
            if nm in dbg_out:
                n = dbg_out[nm].shape[1]
                if t.dtype == BF16:
                    stg = r23(88 * KB, n, F32)
                    act(stg, t, AF.Copy)
                    t = stg
                dma('sp', dbg_out[nm][0:t.parts, 0:t.ncols], t, xsem[0], reads=[t])

        def chk(nm):
            if stop_after == nm:
                raise _Stop()

        try:
            dma('sp', cm, cst_d[:, 640:1664], xsem[0], writes=[cm])
            dma('pool', m2, cst_d[:, 128:640], csem[0], writes=[m2])
            dma('pool', ident, cst_d[:, 0:128], csem[1], writes=[ident])
            dma('sp', smalls, smalls_d[:, :], xsem[3], writes=[smalls])
            dma('sp', wgk2, wgk2_d[:, :], xsem[4], writes=[wgk2])
            dma('pool', wgkl, w_gkl_d[:, :], csem[2], writes=[wgkl])
            tr.op('dve', lambda e: e.memset(ones.ap, 1.0), writes=[ones])
            for t_ in persA + persB:
                tr.op('dve', lambda e, t_=t_: e.memset(t_.ap, 0.0), writes=[t_])
            act(condb, sm(SM_CT, 16), AF.Silu)
            tt(lbt, sm(SM_LB0, 8), sm(SM_LB1, 8), ALU.subtract)
            act(lb, lbt, AF.Sigmoid)
            act(omlb, lb, AF.Identity, bias=1.0, scale=-1.0)
            act(nbgk, sm(SM_BGK, 4), AF.Identity, scale=-1.0)

            sh1 = lambda fc: mod.cols(fc, fc + 1)
            gt1 = lambda fc: mod.cols(32 + fc, 33 + fc)
            sh2 = lambda fc: mod.cols(48 + fc, 49 + fc)
            gt2 = lambda fc: mod.cols(80 + fc, 81 + fc)
            sqringA = [r23(72 * KB, 1024, BF16), r23(74 * KB, 1024, BF16)]
            rsA = r23(76 * KB, 1024, F32)

            def phaseA_loads(s_):
                for fc in range(16):
                    dma('sp', xall[fc], xT[fc, :, s_ * ST:(s_ + 1) * ST], xsem[fc], writes=[xall[fc]])

            pdA = [None]

            def phaseA_ssq_chunk(i):
                if i == 0:
                    pdA[0] = nextpd()
                p = pdA[0]
                sq = sqringA[i % 2]
                act(sq, xall[i], AF.Square)
                for half in range(2):
                    c0, c1 = half * 512, (half + 1) * 512
                    tr.op('pe', lambda e, c0=c0, c1=c1, sq=sq, i=i: e.matmul(p.ap[:, c0:c1], ones.ap, sq.ap[:, c0:c1],
                                                                             start=(i == 0), stop=(i == 15)),
                          reads=[ones, sq.cols(c0, c1)], writes=[p.cols(c0, c1)])

            def phaseA_ssq_finish():
                p = pdA[0]
                act(rsA, p, AF.Ln, bias=EPS, scale=1.0 / D)
                act(rsA, rsA, AF.Exp, scale=-0.5)
                for fc in range(16):
                    tt(xall[fc], xall[fc], rsA, ALU.mult)

            def phaseA_ssq(s_):
                for i in range(16):
                    phaseA_ssq_chunk(i)
                phaseA_ssq_finish()

            def phaseA_scale(s_):
                for fc in range(16):
                    if fc % 2 == 0:
                        act(hT[fc], xall[fc], AF.Identity, bias=sh1(fc), scale=a1.cols(fc, fc + 1))
                    else:
                        ts(hT[fc], xall[fc], a1.cols(fc, fc + 1), sh1(fc), ALU.mult, ALU.add)

            phaseA_loads(0)

            class KeyT:
                def __init__(self, ap, keys):
                    self.ap, self.keys = ap, keys
            rowbuf = [cs(18 * KB, 512, F32, parts=1)]
            m128 = cs(20 * KB, 512, BF16)
            dma('pool', m128, cst_d[:, 1664:2176], csem[0], writes=[m128])
            modraw = cs(17 * KB + 512, 96, F32)
            scr_pm = scr_d.rearrange("(m p) -> p m", p=128)

            def ada_quarter(idx):
                blk, kq = idx // 4, idx % 4
                w = wload(w_ada_d[idx], 2048)
                fns = [(lambda e, j=j, w=w: e.matmul(pa.ap[0:1, 0:512], condb.ap[:, kq * 4 + j: kq * 4 + j + 1],
                                                     w.ap[:, j * 512:(j + 1) * 512],
                                                     start=(kq == 0 and j == 0), stop=(kq == 3 and j == 3)))
                       for j in range(4)]
                tr.op('pe', fns, reads=[w, condb], writes=[pa])
                if kq == 3:
                    rb = rowbuf[0]
                    tr.op('act', lambda e, rb=rb: e.activation(out=rb.ap, in_=pa.ap[0:1, 0:512], func=AF.Copy),
                          reads=[pa], writes=[rb])
                    seg = KeyT(scr_d[blk * 512:(blk + 1) * 512].rearrange("(o n) -> o n", o=1), (("SCR", blk),))
                    dma('sp', seg, rb, xsem[8 + blk % 8], reads=[rb], writes=[seg])

            def ada_readback(m0, m1):
                segs = [KeyT(None, (("SCR", blk),)) for blk in range(m0 // 4, m1 // 4)]
                dst = modraw.cols(m0, m1)
                tr.op('sp', lambda e: e.dma_start(out=dst.ap, in_=scr_pm[:, m0:m1], allow_slow_non_contiguous=True),
                      reads=segs, writes=[dst], dsem=xsem[6])
                tt(mod.cols(m0, m1), dst, sm(SM_BADA + m0, m1 - m0), ALU.add)

            for idx in range(32):
                ada_quarter(idx)
                if idx % 2 == 1:
                    phaseA_ssq_chunk(idx // 2)
            phaseA_ssq_finish()
            ada_readback(0, 32)
            stt(a1, mod.cols(16, 32), 1.0, sm(SM_N1, 16), ALU.add, ALU.mult)
            ada_next = [32]

            def ada_some(n):
                for _ in range(n):
                    idx = ada_next[0]
                    if idx >= 96:
                        return
                    ada_quarter(idx)
                    ada_next[0] = idx + 1
                    if idx == 95:
                        ada_readback(32, 96)
                        stt(a2, mod.cols(64, 80), 1.0, sm(SM_N2, 16), ALU.add, ALU.mult)
                        debug_dump('mod', mod)
            chk('p0')

            class BSet:
                pass

            def mkset(base):
                S = BSet()
                t = [r23(base + i * 2048, 1024, BF16) for i in range(10)]
                S.B0, S.B1, S.B2 = t[0], t[1], t[2]
                S.B3 = [t[3], t[4]]
                S.B4 = [t[5], t[6]]
                S.B5 = t[7]
                S.B6 = [t[8], t[9]]
                S.B7 = r23(base + 20 * KB, 512, BF16)
                S.B8 = [r23(base + 21 * KB, 512, BF16), r23(base + 22 * KB, 512, BF16)]
                return S
            bsets = [mkset(56 * KB), mkset(0)]
            for i_, S_ in enumerate(bsets):
                S_.ebl = r23(31 * KB + i_ * 64, 16, F32)
            o_sb = [r23(23 * KB, 512, F32), r23(25 * KB, 512, F32)]
            rstd_t = r23(27 * KB, 512, F32)
            t_t = r23(29 * KB, 512, F32)
            S_all = r23(79 * KB, 9 * 256, F32)
            S_bf = r23(88 * KB, 8 * 256, BF16)
            F0, F1, F2, F3, F4, F5 = Fw
            gkl = Tl(R23b, "R23", Wb + 5 * 4096, 1024, F32, parts=16)

            def transposes(src, dst, bank='pt'):
                bt, bv = {'pt': (pt, pt_bf), 'po': (po, po_bf), 'ps': (ps, ps_bf)}[bank]
                fns = [(lambda e, b=b: e.transpose(bv[:, b * 128:(b + 1) * 128], src.ap[:, b * 128:(b + 1) * 128], ident.ap))
                       for b in range(8)]
                tr.op('pe', fns, reads=[src, ident], writes=[bt])
                tr.op('act', lambda e: e.activation(out=dst.ap, in_=bv, func=AF.Copy), reads=[bt], writes=[dst])

            def bcast_last(t):
                return t.ap.rearrange("p (c t) -> p c t", t=64)[:, :, 63:64].broadcast_to([128, 16, 64])

            def v3(t):
                return t.ap.rearrange("p (c t) -> p c t", t=64)

            def H(t, hf):
                return t.cols(hf * 512, hf * 512 + 512)

            def v3h(t):
                return t.ap.rearrange("p (c t) -> p c t", t=64)

            def scan_h(dst, src):
                for hf in range(2):
                    d_, s_, c_ = H(dst, hf), H(src, hf), H(cm, hf)
                    tr.op('dve', lambda e, d_=d_, s_=s_, c_=c_: e.tensor_tensor_scan(
                        out=d_.ap, data0=c_.ap, data1=s_.ap, initial=0.0, op0=ALU.mult, op1=ALU.add),
                        reads=[c_, s_], writes=[d_])

            def kdec_and_ebl(S, C=64):
                nch = 512 // C
                for hf in range(2):
                    b1, b0, f3 = H(S.B1, hf), H(S.B0, hf), H(F3, hf)
                    vv = lambda t: t.ap.rearrange("p (c t) -> p c t", t=C)
                    tr.op('dve', lambda e, b1=b1, b0=b0, f3=f3: e.tensor_tensor(
                        out=vv(b1), in0=vv(b0), in1=vv(f3)[:, :, C - 1:C].broadcast_to([128, nch, C]), op=ALU.mult),
                        reads=[b0, f3], writes=[b1])
                    eb = S.ebl.cols(hf * nch, hf * nch + nch)
                    tr.op('dve', lambda e, eb=eb, f3=f3: e.tensor_copy(out=eb.ap.rearrange("p (c o) -> p c o", o=1),
                                                                       in_=vv(f3)[:, :, C - 1:C]),
                          reads=[f3], writes=[eb])

            def cumsum128_fix(t):
                for hf in range(2):
                    th = H(t, hf)
                    v4 = th.ap.rearrange("p (b two t) -> p b two t", two=2, t=64)
                    tr.op('dve', lambda e, v4=v4: e.tensor_tensor(out=v4[:, :, 1, :], in0=v4[:, :, 1, :],
                                                                 in1=v4[:, :, 0, 63:64].broadcast_to([128, 4, 64]), op=ALU.add),
                          reads=[th], writes=[th])

            def tail_transposes(S, ne):
                transposes(S.B1, S.B5)
                for ec in range(ne):
                    transposes(S.B3[ec], S.B6[ec])

            deferred = []

            def flush_deferred():
                while deferred:
                    deferred.pop(0)()

            def tail_gen(S, dv, pers, nw_col, ychunks, C=64):
                ne = dv // 128
                B0, B2, B5, B6, B7, B8 = S.B0, S.B2, S.B5, S.B6, S.B7, S.B8

                nch = 512 // C

                def stage_A(hf):
                    fns = [(lambda e, j=j, blk=4 * hf + j: e.matmul(pa.ap[:, j * 128:(j + 1) * 128],
                                                                    B0.ap[:, blk * 128:(blk + 1) * 128],
                                                                    B2.ap[:, blk * 128:(blk + 1) * 128], start=True, stop=True))
                           for j in range(4)]
                    tr.op('pe', fns, reads=[H(B0, hf), H(B2, hf)], writes=[pa])
                    tt(B7, pa, m2 if C == 64 else m128, ALU.mult)
                    s0 = S_all.cols(0, dv)
                    s_src = pers if hf == 0 else S_all.cols(nch * dv, (nch + 1) * dv)
                    tr.op('dve', lambda e, d_=s0, s_=s_src: e.tensor_copy(out=d_.ap, in_=s_.ap), reads=[s_src], writes=[s0])
                    spb = 512 // dv
                    per_round = 2 * spb
                    banks = [ps, po]
                    nr = nch // per_round
                    for r in range(nr):
                        fns = []
                        rd = []
                        loc = []
                        for k in range(per_round):
                            ci = r * per_round + k
                            c = nch * hf + ci
                            if C == 64:
                                blk, p0, kp = c // 2, 64 * (c % 2), 64
                                bk, sl = banks[c % 2], k // 2
                            else:
                                blk, p0, kp = c, 0, 128
                                bk, sl = banks[k // spb], k % spb
                            loc.append((ci, c, bk, sl))
                            for ec in range(ne):
                                fns.append(lambda e, sl=sl, ec=ec, blk=blk, p0=p0, kp=kp, bk=bk: e.matmul(
                                    bk.ap[:, sl * dv + ec * 128: sl * dv + ec * 128 + 128],
                                    B5.ap[p0:p0 + kp, blk * 128:(blk + 1) * 128],
                                    B6[ec].ap[p0:p0 + kp, blk * 128:(blk + 1) * 128], start=True, stop=True))
                                rd += [B5.cols(blk * 128, blk * 128 + 128), B6[ec].cols(blk * 128, blk * 128 + 128)]
                        tr.op('pe', fns, reads=rd, writes=[ps, po])
                        for (ci, c, bk, sl) in loc:
                            stt(S_all.cols((ci + 1) * dv, (ci + 2) * dv), S_all.cols(ci * dv, (ci + 1) * dv),
                                S.ebl.cols(c, c + 1), bk.cols(sl * dv, (sl + 1) * dv), ALU.mult, ALU.add)
                            if ci == nch // 2 - 1:
                                hh = nch // 2
                                tr.op('dve', lambda e, d_=S_bf.cols(0, hh * dv), s_=S_all.cols(0, hh * dv): e.tensor_copy(out=d_.ap, in_=s_.ap),
                                      reads=[S_all.cols(0, hh * dv)], writes=[S_bf.cols(0, hh * dv)])
                        if r + 1 < nr:
                            yield
                    hh = nch // 2
                    tr.op('dve', lambda e, d_=S_bf.cols(hh * dv, nch * dv), s_=S_all.cols(hh * dv, nch * dv): e.tensor_copy(out=d_.ap, in_=s_.ap),
                          reads=[S_all.cols(hh * dv, nch * dv)], writes=[S_bf.cols(hh * dv, nch * dv)])
                    if hf == 1:
                        tr.op('dve', lambda e, d_=pers, s_=S_all.cols(nch * dv, (nch + 1) * dv): e.tensor_copy(out=d_.ap, in_=s_.ap),
                              reads=[S_all.cols(nch * dv, (nch + 1) * dv)], writes=[pers])

                def stage_O(hf):
                    hh = nch // 2
                    for ec in range(ne):
                        for jh in range(2):
                            fns = []
                            for j in range(2 * jh, 2 * jh + 2):
                                blk = 4 * hf + j
                                fns.append(lambda e, j=j, blk=blk, ec=ec: e.matmul(
                                    pa.ap[:, j * 128:(j + 1) * 128], B6[ec].ap[:, blk * 128:(blk + 1) * 128],
                                    B7.ap[:, j * 128:(j + 1) * 128], start=True, stop=False))
                                npc = 128 // C
                                for cc in range(npc):
                                    ci = j * npc + cc
                                    c0 = blk * 128 + cc * C
                                    fns.append(lambda e, j=j, cc=cc, ci=ci, c0=c0, ec=ec, npc=npc: e.matmul(
                                        pa.ap[:, j * 128 + cc * C: j * 128 + cc * C + C],
                                        S_bf.ap[:, ci * dv + ec * 128: ci * dv + ec * 128 + 128],
                                        B2.ap[:, c0:c0 + C], start=False, stop=(cc == npc - 1)))
                            tr.op('pe', fns, reads=[H(B6[ec], hf), B7.cols(jh * 256, jh * 256 + 256),
                                                    S_bf.cols(jh * hh * dv, (jh + 1) * hh * dv), H(B2, hf)],
                                  writes=[pa.cols(jh * 256, jh * 256 + 256)])
                        tr.op('dve', lambda e, d_=o_sb[ec]: e.tensor_copy(out=d_.ap, in_=pa.ap), reads=[pa], writes=[o_sb[ec]])
                        act(B8[ec], o_sb[ec], AF.Square)

                def stage_N(hf):
                    nb = pt if hf == 0 else ps
                    fns = [(lambda e, ec=ec: e.matmul(nb.ap[:, 0:512], ones.ap, B8[ec].ap, start=(ec == 0), stop=(ec == ne - 1)))
                           for ec in range(ne)]
                    tr.op('pe', fns, reads=[ones] + B8[:ne], writes=[nb])
                    act(rstd_t, nb, AF.Ln, bias=EPS, scale=1.0 / dv)
                    act(rstd_t, rstd_t, AF.Exp, scale=-0.5)

                    def fin():
                        for ec in range(ne):
                            tt(t_t, o_sb[ec], rstd_t, ALU.mult)
                            stt(H(ychunks[ec], hf), t_t, sm(nw_col + ec), H(S.B4[ec], hf), ALU.mult, ALU.mult)
                    return fin

                for _ in stage_A(0):
                    yield
                flush_deferred()
                yield
                stage_O(0)
                yield
                fin0 = stage_N(0)
                for _ in stage_A(1):
                    yield
                fin0()
                yield
                stage_O(1)
                yield
                deferred.append(stage_N(1))
                yield

            def frontA_gen(h, S):
                p = dense(w_in_d[8 + h], hT)
                for hf in range(2):
                    act(H(F0, hf), H(p, hf), AF.Sigmoid)
                for hf in range(2):
                    ts(H(F0, hf), H(F0, hf), omlb.cols(h, h + 1), lb.cols(h, h + 1), ALU.mult, ALU.add)
                for hf in range(2):
                    ts(H(F1, hf), H(F0, hf), -1.0, 1.0, ALU.mult, ALU.add)
                for hf in range(2):
                    act(H(F0, hf), H(F0, hf), AF.Ln)
                yield
                scan_h(F2, F0)
                for hf in range(2):
                    act(H(F3, hf), H(F2, hf), AF.Exp)
                for hf in range(2):
                    act(H(F4, hf), H(F2, hf), AF.Exp, scale=-1.0)
                for hf in range(2):
                    tt(H(S.B0, hf), H(F1, hf), H(F4, hf), ALU.mult)
                kdec_and_ebl(S)
                p = dense(w_in_d[0 + h], hT)
                for hf in range(2):
                    act(H(F5, hf), H(p, hf), AF.Silu)
                for hf in range(2):
                    tt(H(S.B2, hf), H(F5, hf), H(F3, hf), ALU.mult)
                yield
                p = dense(w_in_d[16 + h], hT)
                for hf in range(2):
                    tr.op('dve', lambda e, d_=H(S.B3[0], hf), p_=H(p, hf): e.tensor_copy(out=d_.ap, in_=p_.ap), reads=[H(p, hf)], writes=[H(S.B3[0], hf)])
                yield
                transposes(S.B1, S.B5, 'pt')
                transposes(S.B3[0], S.B6[0], 'po')
                p = dense(w_in_d[24 + h], hT)
                yield
                for hf in range(2):
                    act(H(S.B4[0], hf), H(p, hf), AF.Silu)
                yield

            def gk_low():
                p = nextpd()
                for half in range(2):
                    c0, c1 = half * 512, (half + 1) * 512
                    fns = [(lambda e, kc=kc, c0=c0, c1=c1: e.matmul(p.ap[0:16, c0:c1], wgkl.ap[:, kc * 16:(kc + 1) * 16],
                                                                    hT[kc].ap[:, c0:c1], start=(kc == 0), stop=(kc == 15)))
                           for kc in range(16)]
                    tr.op('pe', fns, reads=[wgkl] + [a.cols(c0, c1) for a in hT], writes=[p.cols(c0, c1)])
                tr.op('act', lambda e: e.activation(out=gkl.ap, in_=p.ap[0:16, :], func=AF.Copy), reads=[p], writes=[gkl])

            def frontB_gen(h, S):
                if h == 0:
                    gk_low()
                p = nextpd()
                for half in range(2):
                    c0, c1 = half * 512, (half + 1) * 512
                    tr.op('pe', lambda e, c0=c0, c1=c1: e.matmul(p.ap[:, c0:c1], wgk2.ap[0:16, h * 128:(h + 1) * 128],
                                                                 gkl.ap[0:16, c0:c1], start=True, stop=True),
                          reads=[wgk2, gkl], writes=[p.cols(c0, c1)])
                for hf in range(2):
                    act(H(F0, hf), H(p, hf), AF.Exp, bias=nbgk.cols(h, h + 1), scale=-1.0)
                for hf in range(2):
                    act(H(F0, hf), H(F0, hf), AF.Ln, bias=1.0)
                yield
                scan_h(F2, F0)
                cumsum128_fix(F2)
                for hf in range(2):
                    act(H(F3, hf), H(F2, hf), AF.Exp, scale=-1.0 / 16.0)
                for hf in range(2):
                    act(H(F4, hf), H(F2, hf), AF.Exp, scale=1.0 / 16.0)
                p = dense(w_in_d[36 + h], hT)
                for hf in range(2):
                    tt(H(S.B0, hf), H(p, hf), H(F4, hf), ALU.mult)
                kdec_and_ebl(S, C=128)
                yield
                p = dense(w_in_d[32 + h], hT)
                for hf in range(2):
                    stt(H(S.B2, hf), H(p, hf), 128.0 ** -0.5, H(F3, hf), ALU.mult, ALU.mult)
                yield
                p = dense(w_in_d[40 + 2 * h], hT)
                for hf in range(2):
                    tr.op('dve', lambda e, d_=H(S.B3[0], hf), p_=H(p, hf): e.tensor_copy(out=d_.ap, in_=p_.ap), reads=[H(p, hf)], writes=[H(S.B3[0], hf)])
                yield
                p = dense(w_in_d[41 + 2 * h], hT)
                for hf in range(2):
                    tr.op('dve', lambda e, d_=H(S.B3[1], hf), p_=H(p, hf): e.tensor_copy(out=d_.ap, in_=p_.ap), reads=[H(p, hf)], writes=[H(S.B3[1], hf)])
                yield
                p = dense(w_in_d[48 + 2 * h], hT)
                for hf in range(2):
                    act(H(S.B4[0], hf), H(p, hf), AF.Silu)
                yield
                transposes(S.B1, S.B5, 'pt')
                transposes(S.B3[0], S.B6[0], 'po')
                transposes(S.B3[1], S.B6[1], 'ps')
                p = dense(w_in_d[49 + 2 * h], hT)
                yield
                for hf in range(2):
                    act(H(S.B4[1], hf), H(p, hf), AF.Silu)
                yield

            def step(g):
                if g is None:
                    return False
                try:
                    next(g)
                    return True
                except StopIteration:
                    return False

            for s in range(NST):
                tok = slice(s * ST, (s + 1) * ST)
                sqring = [r23(64 * KB, 1024, BF16), r23(66 * KB, 1024, BF16)]
                rs = r23(68 * KB, 1024, F32)
                nbanks = [pt, ps]
                phaseA_scale(s)
                fine_next[0] = True
                if s == 0:
                    debug_dump("hT0", hT[0])
                    chk('A')

                heads = []
                for h in range(8):
                    heads.append(('A', h))
                for h in range(4):
                    heads.append(('B', h))
                tg = None
                for i, (kind, h) in enumerate(heads):
                    S = bsets[(i + 1) % 2]
                    if kind == 'A':
                        fg = frontA_gen(h, S)
                        mk_tail = (lambda S=S, h=h: tail_gen(S, 128, persA[h], SM_ANW, [yT[h]]))
                    else:
                        fg = frontB_gen(h, S)
                        mk_tail = (lambda S=S, h=h: tail_gen(S, 256, persB[h], SM_BNW, [yT[8 + 2 * h], yT[9 + 2 * h]], C=128))
                    while True:
                        a_ = step(tg)
                        b_ = step(fg)
                        if not a_ and not b_:
                            break
                    tg = mk_tail()
                def phaseC_gen(fc):
                    p = dense(w_in_d[56 + fc], hT)
                    act(F0, p, AF.Sigmoid)
                    if s == 0 and fc > 0:
                        ada_some(2 if fc == 1 else 1)
                    yield
                    p = dense(w_upa_d[fc], yT[0:8])
                    tt(F1, p, F0, ALU.mult)
                    if s == 0 and fc > 0:
                        ada_some(2 if fc == 1 else 1)
                    yield
                    p = dense(w_in_d[72 + fc], hT)
                    act(F2, p, AF.Sigmoid)
                    if s == 0 and fc > 0:
                        ada_some(2 if fc == 1 else 1)
                    yield
                    p = dense(w_upb_d[fc], yT[8:16])
                    tt(F3, p, F2, ALU.mult)
                    tt(mg[fc], F1, F3, ALU.add)
                    if s == 0 and fc > 0:
                        ada_some(2 if fc == 1 else 1)
                    yield
                for fc in range(16):
                    cg = phaseC_gen(fc)
                    if fc == 0:
                        step(tg); step(cg)
                        step(tg); step(cg)
                        while step(tg):
                            pass
                        flush_deferred()
                    while step(cg):
                        pass
                if s == 0:
                    debug_dump("yA0", yT[0])
                    debug_dump("yB0", yT[8])
                if s == 0:
                    debug_dump("mg0", mg[0])
                    chk('C')

                for fc in range(16):
                    dma('sp', xres[fc], xT[fc, :, tok], xsem[fc], writes=[xres[fc]])
                    p = dense(w_o_d[fc], mg)
                    stt(xres[fc], p, gt1(fc), xres[fc], ALU.mult, ALU.add)
                    if fc > 0:
                        ssq_accum(xres[fc - 1], fc - 1, 16, sqring, nbanks)
                ssq_accum(xres[15], 15, 16, sqring, nbanks)
                if s == 0:
                    debug_dump("x1_0", xres[0])
                    chk('D')
                ssq_finish(rs, nbanks)
                tmpr = [r23(72 * KB, 1024, F32), r23(76 * KB, 1024, F32)]
                for fc in range(16):
                    tm = tmpr[fc % 2]
                    tt(tm, xres[fc], rs, ALU.mult)
                    act(mg[fc], tm, AF.Identity, bias=sh2(fc), scale=a2.cols(fc, fc + 1))
                fine_next[0] = True

                actT = [r23(Wb + j * 2048, 1024, BF16) for j in range(11)]
                sgr = [r23(56 * KB, 1024, F32), r23(60 * KB, 1024, F32)]
                for q in range(4):
                    for j in range(11):
                        hc = q * 11 + j
                        sg = sgr[j % 2]
                        p = dense(w_fi_d[hc], mg)
                        act(sg, p, AF.Silu)
                        p = dense(w_fi_d[44 + hc], mg)
                        tt(actT[j], p, sg, ALU.mult)
                    for fc in range(16):
                        p = dense(w_fo_d[fc, :, q * 1408:(q + 1) * 1408], actT)
                        stt(xres[fc], p, gt2(fc), xres[fc], ALU.mult, ALU.add)
                        if q == 3 and fc > 0:
                            ssq_accum(xres[fc - 1], fc - 1, 16, sqring, nbanks)
                ssq_accum(xres[15], 15, 16, sqring, nbanks)

                if s + 1 < NST:
                    phaseA_loads(s + 1)
                ssq_finish(rs, nbanks)
                for fc in range(16):
                    stt(xres[fc], xres[fc], sm(SM_NF + fc), rs, ALU.mult, ALU.mult)
                    dma('sp', outT[fc, :, tok], xres[fc], osem[fc], reads=[xres[fc]])
                if s + 1 < NST:
                    phaseA_ssq(s + 1)


        except _Stop:
            pass
        for sx in xsem + osem:
            if sx.count > 0:
                tr._wait('sp', sx, sx.count)
        print("instructions:", tr.ninst, "sems:", tr.nsem, "pe:", tr.cnt['pe'], "act:", tr.cnt['act'], "dve:", tr.cnt['dve'])
    return nc


def _colchunks(W):
    K, N = W.shape
    return np.ascontiguousarray(W.reshape(K // 128, 128, N // 128, 128).transpose(2, 1, 0, 3)).reshape(N // 128, 128, (K // 128) * 128)


def _pcol(v):
    return np.ascontiguousarray(v.reshape(-1, 128).T)


def make_consts():
    ident = np.eye(128, dtype=np.float32)
    j = np.arange(128)[:, None]
    i = np.arange(128)[None, :]
    m2 = ((j // 64 == i // 64) & (j <= i)).astype(np.float32)
    m2t = np.tile(m2, (1, 4))
    cmv = np.ones((128, 1024), np.float32)
    cmv[:, ::64] = 0.0
    m128t = np.tile((j <= i).astype(np.float32), (1, 4))
    return np.concatenate([ident, m2t, cmv, m128t], axis=1)


def prep_inputs(x, c, w_ada, b_ada, norm1_w, w_in, w_gk2, b_gk2, lb_param, a_norm_w, b_norm_w,
                w_up_a, w_up_b, w_o, norm2_w, w_ffn_in, w_ffn_out, final_norm_w):
    f = lambda a: np.asarray(a, dtype=np.float32)
    x, c = f(x), f(c)
    w_in0 = f(w_in)[0]
    w_in_ng = np.concatenate([w_in0[:, :7168], w_in0[:, 7184:]], axis=1)
    shared = {
        "wgk2": np.ascontiguousarray(f(w_gk2)[0]),
        "cst": make_consts(),
        "w_ada_l": np.ascontiguousarray(f(w_ada)[0].reshape(4, 4, 128, 24, 512).transpose(3, 0, 2, 1, 4)).reshape(96, 128, 2048),
        "w_in_l": _colchunks(w_in_ng),
        "w_gkl": np.ascontiguousarray(w_in0[:, 7168:7184].reshape(16, 128, 16).transpose(1, 0, 2)).reshape(128, 256),
        "w_upa_l": _colchunks(f(w_up_a)[0]),
        "w_upb_l": _colchunks(f(w_up_b)[0]),
        "w_o_l": _colchunks(f(w_o)[0]),
        "w_fi_l": _colchunks(f(w_ffn_in)[0]),
        "w_fo_l": _colchunks(f(w_ffn_out)[0]),
    }
    sm_common = np.zeros((128, NSM), np.float32)
    sm_common[:, SM_BADA:SM_BADA + 96] = _pcol(f(b_ada)[0])
    sm_common[:, SM_N1:SM_N1 + 16] = _pcol(f(norm1_w)[0])
    sm_common[:, SM_N2:SM_N2 + 16] = _pcol(f(norm2_w)[0])
    sm_common[:, SM_NF:SM_NF + 16] = _pcol(f(final_norm_w))
    sm_common[:, SM_LB0:SM_LB0 + 8] = _pcol(f(lb_param)[0])
    sm_common[:, SM_LB1:SM_LB1 + 8] = _pcol(f(lb_param)[1])
    sm_common[:, SM_ANW:SM_ANW + 1] = _pcol(f(a_norm_w)[0])
    sm_common[:, SM_BNW:SM_BNW + 2] = _pcol(f(b_norm_w)[0])
    sm_common[:, SM_BGK:SM_BGK + 4] = _pcol(f(b_gk2)[0])
    in_maps = []
    for b in range(8):
        smb = sm_common.copy()
        smb[:, SM_CT:SM_CT + 16] = _pcol(c[b])
        m = dict(shared)
        m["smalls"] = smb
        m["xT"] = np.ascontiguousarray(x[b].T).reshape(16, 128, T)
        in_maps.append(m)
    return in_maps


_NC_CACHE = {}


def kernel(**inputs):
    in_maps = prep_inputs(**inputs)
    if "nc" not in _NC_CACHE:
        _NC_CACHE["nc"] = build_nc()
    nc = _NC_CACHE["nc"]
    res = run_bass_kernel_spmd(nc, in_maps, core_ids=list(range(8)))
    out = np.empty((8, T, D), np.float32)
    for b in range(8):
        out[b] = res.results[b]["outT"].reshape(D, T).T
    return out
```

```python
import numpy as np
import concourse.bass as bass
import concourse.mybir as mybir
from concourse.bass_utils import run_bass_kernel_spmd
from contextlib import ExitStack

F32 = mybir.dt.float32
BF16 = mybir.dt.bfloat16
AF = mybir.ActivationFunctionType
ALU = mybir.AluOpType

D = 2048
T = 2048
ST = 1024
NST = T // ST
KC = 16
EPS = 1e-6
UNIT = 512
EPOCH = 30000
NSLOT = 7

SM_BADA = 0
SM_N1 = 96
SM_N2 = 112
SM_NF = 128
SM_LB0 = 144
SM_LB1 = 152
SM_ANW = 160
SM_BNW = 161
SM_BGK = 163
SM_CT = 167
NSM = 183


class Sem:
    def __init__(self, h):
        self.h = h
        self.count = 0


class Tl:
    def __init__(self, base, name, off, ncols, dtype, parts=128, p0=0):
        self.base = base
        self.name = name
        self.off = off
        self.ncols = ncols
        self.dtype = dtype
        self.parts = parts
        self.p0 = p0
        self.esz = 2 if dtype == BF16 else 4
        size = ncols * self.esz
        assert off % 4 == 0 and size % 4 == 0, (off, size)
        ap = base[p0:p0 + parts, off // 4:(off + size) // 4]
        if dtype == BF16:
            ap = ap.bitcast(BF16)
        self.ap = ap
        self.keys = tuple((name, u) for u in range(off // UNIT, (off + size - 1) // UNIT + 1))

    def cols(self, lo, hi, p0=None, parts=None):
        p0 = self.p0 if p0 is None else p0
        parts = self.parts if parts is None else parts
        return Tl(self.base, self.name, self.off + lo * self.esz, hi - lo, self.dtype, parts, p0)


class Tracker:
    def __init__(self, nc, es):
        self.nc = nc
        self.es = es
        self.eng = {'pe': nc.tensor, 'act': nc.scalar, 'dve': nc.vector, 'pool': nc.gpsimd, 'sp': nc.sync}
        self.cnt = {e: 0 for e in self.eng}
        self.esems = {e: [] for e in self.eng}
        self.known = {e: {} for e in self.eng}
        self.lastw = {}
        self.readers = {}
        self.nsem = 0
        self.ninst = 0

    def newsem(self, name):
        self.nsem += 1
        return Sem(self.es.enter_context(self.nc.semaphore(f"{name}_{self.nsem}")))

    def _cursem(self, e):
        ep = self.cnt[e] // EPOCH
        while len(self.esems[e]) <= ep:
            self.esems[e].append(self.newsem(f"e_{e}"))
        return self.esems[e][ep]

    def _wait(self, e, sem, val):
        if self.known[e].get(sem, 0) < val:
            self.eng[e].wait_ge(sem.h, val)
            self.known[e][sem] = val
            self.ninst += 1

    def op(self, e, fns, reads=(), writes=(), dsem=None):
        if not isinstance(fns, (list, tuple)):
            fns = [fns]
        deps = {}

        def add(ev):
            s, v = ev
            if deps.get(s, 0) < v:
                deps[s] = v
        for t in reads:
            for k in t.keys:
                w = self.lastw.get(k)
                if w:
                    add(w)
        for t in writes:
            for k in t.keys:
                w = self.lastw.get(k)
                if w:
                    add(w)
                for ev in self.readers.get(k, {}).items():
                    add(ev)
        own = self._cursem(e) if dsem is None else None
        for s, v in deps.items():
            if e == 'pe' and s in self.esems['pe']:
                continue
            self._wait(e, s, v)
        if dsem is not None and dsem.count > 0:
            self._wait(e, dsem, dsem.count)
        ins = None
        for fn in fns:
            ins = fn(self.eng[e])
            self.ninst += 1
        if dsem is None:
            self.cnt[e] += 1
            local = (self.cnt[e] - 1) % EPOCH + 1
            ins.then_inc(own.h, 1)
            ev = (own, local)
        else:
            dsem.count += 16
            ins.then_inc(dsem.h, 16)
            ev = (dsem, dsem.count)
        for t in reads:
            for k in t.keys:
                r = self.readers.setdefault(k, {})
                if r.get(ev[0], 0) < ev[1]:
                    r[ev[0]] = ev[1]
        for t in writes:
            for k in t.keys:
                self.lastw[k] = ev
                self.readers[k] = {}
        return ev

    def finish(self):
        pass


class _Stop(Exception):
    pass


def build_nc(dbg=None, stop_after=None):
    nc = bass.Bass("TRN2", target_bir_lowering=False)

    def din(name, shape):
        return nc.dram_tensor(name, list(shape), F32, kind="ExternalInput").ap()
    xT = din("xT", [16, 128, T])
    smalls_d = din("smalls", [128, NSM])
    wgk2_d = din("wgk2", [16, 512])
    cst_d = din("cst", [128, 128 + 512 + 1024 + 512])
    w_ada_d = din("w_ada_l", [96, 128, 2048])
    w_in_d = din("w_in_l", [88, 128, 2048])
    w_gkl_d = din("w_gkl", [128, 256])
    w_upa_d = din("w_upa_l", [16, 128, 1024])
    w_upb_d = din("w_upb_l", [16, 128, 1024])
    w_o_d = din("w_o_l", [16, 128, 2048])
    w_fi_d = din("w_fi_l", [88, 128, 2048])
    w_fo_d = din("w_fo_l", [16, 128, 5632])
    outT = nc.dram_tensor("outT", [16, 128, T], F32, kind="ExternalOutput").ap()
    scr_d = nc.dram_tensor("scr", [96 * 128], F32, kind="Internal").ap()
    dbg_out = {}
    if dbg:
        for nm, ncols in dbg.items():
            dbg_out[nm] = nc.dram_tensor("dbg_" + nm, [128, ncols], F32, kind="ExternalOutput").ap()

    with ExitStack() as es:
        tr = Tracker(nc, es)
        KB = 1024
        R1b = es.enter_context(nc.sbuf_tensor("R1", [128, 64 * KB // 4], F32))
        R23b = es.enter_context(nc.sbuf_tensor("R23", [128, 92 * KB // 4], F32))
        WRb = es.enter_context(nc.sbuf_tensor("WR", [128, NSLOT * 4 * KB // 4], F32))
        CSb = es.enter_context(nc.sbuf_tensor("CS", [128, 22 * KB // 4], F32))
        PD = [es.enter_context(nc.psum_tensor(f"PD{i}", [128, 1024], F32)) for i in range(2)]
        PTb = es.enter_context(nc.psum_tensor("PT", [128, 512], F32))
        PAb = es.enter_context(nc.psum_tensor("PA", [128, 512], F32))
        PSb = es.enter_context(nc.psum_tensor("PS", [128, 512], F32))
        POb = es.enter_context(nc.psum_tensor("PO", [128, 512], F32))

        class PsT(Tl):
            def __init__(self, base, name, off, ncols, dtype, parts=128, p0=0):
                self.base, self.name, self.off, self.ncols, self.dtype = base, name, off, ncols, dtype
                self.parts, self.p0 = parts, p0
                self.esz = 2 if dtype == BF16 else 4
                lo = off // self.esz
                self.ap = base[p0:p0 + parts, lo:lo + ncols]
                size = ncols * self.esz
                self.keys = tuple((name, u) for u in range(off // UNIT, (off + size - 1) // UNIT + 1))

            def cols(self, lo, hi, p0=None, parts=None):
                p0 = self.p0 if p0 is None else p0
                parts = self.parts if parts is None else parts
                return PsT(self.base, self.name, self.off + lo * self.esz, hi - lo, self.dtype, parts, p0)

        pd = [PsT(PD[i], f"PD{i}", 0, 1024, F32) for i in range(2)]
        pt = PsT(PTb, "PT", 0, 512, F32)
        pt_bf = PTb[:, :].bitcast(BF16)
        pa = PsT(PAb, "PA", 0, 512, F32)
        ps = PsT(PSb, "PS", 0, 512, F32)
        po = PsT(POb, "PO", 0, 512, F32)

        def r1(off, n, dt):
            return Tl(R1b, "R1", off, n, dt)

        def r23(off, n, dt, parts=128):
            return Tl(R23b, "R23", off, n, dt, parts)

        def cs(off, n, dt, parts=128):
            return Tl(CSb, "CS", off, n, dt, parts)

        o = 0
        cm = cs(o, 1024, F32); o += 4096
        m2 = cs(o, 512, BF16); o += 1024
        ident = cs(o, 128, BF16); o += 256
        ones = cs(o, 128, BF16); o += 256
        smalls = cs(o, NSM, F32); o += 768
        mod = cs(o, 96, F32); o += 384
        a1 = cs(o, 16, F32); o += 64
        a2 = cs(o, 16, F32); o += 64
        lb = cs(o, 8, F32); o += 32
        omlb = cs(o, 8, F32); o += 32
        nbgk = cs(o, 4, F32); o += 16
        condb = cs(o, 16, BF16); o += 32
        lbt = cs(o, 8, F32); o += 32
        wgk2 = cs(o, 512, F32, parts=16); o += 2048
        wgkl = cs(o, 256, BF16); o += 512
        persA = [cs(o + h * 512, 128, F32) for h in range(8)]; o += 4096
        persB = [cs(o + h * 1024, 256, F32) for h in range(4)]; o += 4096
        assert o <= 17 * KB + 512

        def sm(c0, n=1):
            return smalls.cols(c0, c0 + n)

        hT = [r1(kc * 2048, 1024, BF16) for kc in range(16)]
        yT = [r1(32 * KB + c * 2048, 1024, BF16) for c in range(16)]
        xres = [r1(fc * 4096, 1024, F32) for fc in range(16)]
        mg = [r23(c * 2048, 1024, BF16) for c in range(16)]
        xall = [r23(fc * 4096, 1024, F32) for fc in range(16)]
        Wb = 32 * KB
        Fw = [r23(Wb + i * 4096, 1024, F32) for i in range(6)]

        xsem = [tr.newsem("x") for _ in range(16)]
        wsem = [tr.newsem("w") for _ in range(NSLOT)]
        csem = [tr.newsem("c") for _ in range(3)]
        osem = [tr.newsem("o") for _ in range(16)]
        wslot = [0]
        pdi = [0]

        def act(out, in_, func, bias=None, scale=None, reads=(), extra_w=()):
            kw = {}
            if bias is not None:
                kw['bias'] = bias.ap if isinstance(bias, Tl) else bias
            if scale is not None:
                kw['scale'] = scale.ap if isinstance(scale, Tl) else scale
            rd = [in_] + [x for x in (bias, scale) if isinstance(x, Tl)] + list(reads)
            tr.op('act', lambda e: e.activation(out=out.ap, in_=in_.ap, func=func, **kw), reads=rd, writes=[out])

        def tt(out, in0, in1, op, eng='dve'):
            tr.op(eng, lambda e: e.tensor_tensor(out=out.ap, in0=in0.ap, in1=in1.ap, op=op), reads=[in0, in1], writes=[out])

        def ts(out, in0, s1, s2, op0, op1):
            rd = [in0] + [x for x in (s1, s2) if isinstance(x, Tl)]
            a1_ = s1.ap if isinstance(s1, Tl) else s1
            a2_ = s2.ap if isinstance(s2, Tl) else s2
            tr.op('dve', lambda e: e.tensor_scalar(out=out.ap, in0=in0.ap, scalar1=a1_, scalar2=a2_, op0=op0, op1=op1),
                  reads=rd, writes=[out])

        def stt(out, in0, scalar, in1, op0, op1):
            rd = [in0, in1] + ([scalar] if isinstance(scalar, Tl) else [])
            sc = scalar.ap if isinstance(scalar, Tl) else scalar
            tr.op('dve', lambda e: e.scalar_tensor_tensor(out=out.ap, in0=in0.ap, scalar=sc, in1=in1.ap, op0=op0, op1=op1),
                  reads=rd, writes=[out])

        def dma(q, out, in_, sem, reads=(), writes=()):
            o_ap = out.ap if hasattr(out, 'keys') else out
            i_ap = in_.ap if hasattr(in_, 'keys') else in_
            tr.op(q, lambda e: e.dma_start(out=o_ap, in_=i_ap), reads=list(reads), writes=list(writes), dsem=sem)

        def wload(src, ncols):
            i = wslot[0] % NSLOT
            wslot[0] += 1
            t = Tl(WRb, "WR", i * 4096, ncols, BF16)
            full = Tl(WRb, "WR", i * 4096, 2048, BF16)
            dma('pool', t, src, wsem[i], writes=[full])
            return t

        def nextpd():
            p = pd[pdi[0] % 2]
            pdi[0] += 1
            return p

        fine_next = [False]

        def dense(src, acts, ncols=128):
            nk = len(acts)
            w = wload(src, nk * ncols)
            p = nextpd()
            fine = fine_next[0]
            fine_next[0] = False
            for half in range(2):
                c0, c1 = half * 512, (half + 1) * 512
                if fine:
                    for kc in range(nk):
                        tr.op('pe', lambda e, kc=kc, c0=c0, c1=c1: e.matmul(
                            p.ap[0:ncols, c0:c1], w.ap[:, kc * ncols:(kc + 1) * ncols], acts[kc].ap[:, c0:c1],
                            start=(kc == 0), stop=(kc == nk - 1)),
                            reads=[w, acts[kc].cols(c0, c1)], writes=[p.cols(c0, c1)])
                    continue
                fns = [(lambda e, kc=kc: e.matmul(p.ap[0:ncols, c0:c1], w.ap[:, kc * ncols:(kc + 1) * ncols],
                                                  acts[kc].ap[:, c0:c1], start=(kc == 0), stop=(kc == nk - 1)))
                       for kc in range(nk)]
                tr.op('pe', fns, reads=[w] + [a.cols(c0, c1) for a in acts], writes=[p.cols(c0, c1)])
            return p

        def ssq_norm(chunks, sqring, rs):
            p = nextpd()
            n = len(chunks)
            for i, xc in enumerate(chunks):
                sq = sqring[i % len(sqring)]
                act(sq, xc, AF.Square)
                for half in range(2):
                    c0, c1 = half * 512, (half + 1) * 512
                    tr.op('pe', lambda e, c0=c0, c1=c1, sq=sq, i=i: e.matmul(p.ap[:, c0:c1], ones.ap, sq.ap[:, c0:c1],
                                                                             start=(i == 0), stop=(i == n - 1)),
                          reads=[ones, sq.cols(c0, c1)], writes=[p.cols(c0, c1)])
            act(rs, p, AF.Ln, bias=EPS, scale=1.0 / D)
            act(rs, rs, AF.Exp, scale=-0.5)

        def ssq_accum(xc, i, n, sqring, banks):
            sq = sqring[i % len(sqring)]
            act(sq, xc, AF.Square)
            for half in range(2):
                c0, c1 = half * 512, (half + 1) * 512
                bk = banks[half]
                tr.op('pe', lambda e, c0=c0, c1=c1, sq=sq, bk=bk: e.matmul(bk.ap[:, 0:512], ones.ap, sq.ap[:, c0:c1],
                                                                         start=(i == 0), stop=(i == n - 1)),
                      reads=[ones, sq.cols(c0, c1)], writes=[bk])

        def ssq_finish(rs, banks):
            for half in range(2):
                act(rs.cols(half * 512, half * 512 + 512), banks[half], AF.Ln, bias=EPS, scale=1.0 / D)
            act(rs, rs, AF.Exp, scale=-0.5)

        def debug_dump(nm, t):
            if nm in dbg_out:
                n = dbg_out[nm].shape[1]
                if t.dtype == BF16:
                    stg = r23(88 * KB, n, F32)
                    act(stg, t, AF.Copy)
                    t = stg
                dma('sp', dbg_out[nm][0:t.parts, 0:t.ncols], t, xsem[0], reads=[t])

        def chk(nm):
            if stop_after == nm:
                raise _Stop()

        try:
            dma('sp', cm, cst_d[:, 640:1664], xsem[0], writes=[cm])
            dma('pool', m2, cst_d[:, 128:640], csem[0], writes=[m2])
            dma('pool', ident, cst_d[:, 0:128], csem[1], writes=[ident])
            dma('sp', smalls, smalls_d[:, :], xsem[3], writes=[smalls])
            dma('sp', wgk2, wgk2_d[:, :], xsem[4], writes=[wgk2])
            dma('pool', wgkl, w_gkl_d[:, :], csem[2], writes=[wgkl])
            tr.op('dve', lambda e: e.memset(ones.ap, 1.0), writes=[ones])
            for t_ in persA + persB:
                tr.op('dve', lambda e, t_=t_: e.memset(t_.ap, 0.0), writes=[t_])
            act(condb, sm(SM_CT, 16), AF.Silu)
            tt(lbt, sm(SM_LB0, 8), sm(SM_LB1, 8), ALU.subtract)
            act(lb, lbt, AF.Sigmoid)
            act(omlb, lb, AF.Identity, bias=1.0, scale=-1.0)
            act(nbgk, sm(SM_BGK, 4), AF.Identity, scale=-1.0)

            sh1 = lambda fc: mod.cols(fc, fc + 1)
            gt1 = lambda fc: mod.cols(32 + fc, 33 + fc)
            sh2 = lambda fc: mod.cols(48 + fc, 49 + fc)
            gt2 = lambda fc: mod.cols(80 + fc, 81 + fc)
            sqringA = [r23(72 * KB, 1024, BF16), r23(74 * KB, 1024, BF16)]
            rsA = r23(76 * KB, 1024, F32)

            def phaseA_loads(s_):
                for fc in range(16):
                    dma('sp', xall[fc], xT[fc, :, s_ * ST:(s_ + 1) * ST], xsem[fc], writes=[xall[fc]])

            pdA = [None]

            def phaseA_ssq_chunk(i):
                if i == 0:
                    pdA[0] = nextpd()
                p = pdA[0]
                sq = sqringA[i % 2]
                act(sq, xall[i], AF.Square)
                for half in range(2):
                    c0, c1 = half * 512, (half + 1) * 512
                    tr.op('pe', lambda e, c0=c0, c1=c1, sq=sq, i=i: e.matmul(p.ap[:, c0:c1], ones.ap, sq.ap[:, c0:c1],
                                                                             start=(i == 0), stop=(i == 15)),
                          reads=[ones, sq.cols(c0, c1)], writes=[p.cols(c0, c1)])

            def phaseA_ssq_finish():
                p = pdA[0]
                act(rsA, p, AF.Ln, bias=EPS, scale=1.0 / D)
                act(rsA, rsA, AF.Exp, scale=-0.5)
                for fc in range(16):
                    tt(xall[fc], xall[fc], rsA, ALU.mult)

            def phaseA_ssq(s_):
                for i in range(16):
                    phaseA_ssq_chunk(i)
                phaseA_ssq_finish()

            def phaseA_scale(s_):
                for fc in range(16):
                    if fc % 2 == 0:
                        act(hT[fc], xall[fc], AF.Identity, bias=sh1(fc), scale=a1.cols(fc, fc + 1))
                    else:
                        ts(hT[fc], xall[fc], a1.cols(fc, fc + 1), sh1(fc), ALU.mult, ALU.add)

            phaseA_loads(0)

            class KeyT:
                def __init__(self, ap, keys):
                    self.ap, self.keys = ap, keys
            rowbuf = [cs(18 * KB, 512, F32, parts=1)]
            m128 = cs(20 * KB, 512, BF16)
            dma('pool', m128, cst_d[:, 1664:2176], csem[0], writes=[m128])
            modraw = cs(17 * KB + 512, 96, F32)
            scr_pm = scr_d.rearrange("(m p) -> p m", p=128)

            def ada_quarter(idx):
                blk, kq = idx // 4, idx % 4
                w = wload(w_ada_d[idx], 2048)
                fns = [(lambda e, j=j, w=w: e.matmul(pa.ap[0:1, 0:512], condb.ap[:, kq * 4 + j: kq * 4 + j + 1],
                                                     w.ap[:, j * 512:(j + 1) * 512],
                                                     start=(kq == 0 and j == 0), stop=(kq == 3 and j == 3)))
                       for j in range(4)]
                tr.op('pe', fns, reads=[w, condb], writes=[pa])
                if kq == 3:
                    rb = rowbuf[0]
                    tr.op('act', lambda e, rb=rb: e.activation(out=rb.ap, in_=pa.ap[0:1, 0:512], func=AF.Copy),
                          reads=[pa], writes=[rb])
                    seg = KeyT(scr_d[blk * 512:(blk + 1) * 512].rearrange("(o n) -> o n", o=1), (("SCR", blk),))
                    dma('sp', seg, rb, xsem[8 + blk % 8], reads=[rb], writes=[seg])

            def ada_readback(m0, m1):
                segs = [KeyT(None, (("SCR", blk),)) for blk in range(m0 // 4, m1 // 4)]
                dst = modraw.cols(m0, m1)
                tr.op('sp', lambda e: e.dma_start(out=dst.ap, in_=scr_pm[:, m0:m1], allow_slow_non_contiguous=True),
                      reads=segs, writes=[dst], dsem=xsem[6])
                tt(mod.cols(m0, m1), dst, sm(SM_BADA + m0, m1 - m0), ALU.add)

            for idx in range(32):
                ada_quarter(idx)
                if idx % 2 == 1:
                    phaseA_ssq_chunk(idx // 2)
            phaseA_ssq_finish()
            ada_readback(0, 32)
            stt(a1, mod.cols(16, 32), 1.0, sm(SM_N1, 16), ALU.add, ALU.mult)
            ada_next = [32]

            def ada_some(n):
                for _ in range(n):
                    idx = ada_next[0]
                    if idx >= 96:
                        return
                    ada_quarter(idx)
                    ada_next[0] = idx + 1
                    if idx == 95:
                        ada_readback(32, 96)
                        stt(a2, mod.cols(64, 80), 1.0, sm(SM_N2, 16), ALU.add, ALU.mult)
                        debug_dump('mod', mod)
            chk('p0')

            class BSet:
                pass

            def mkset(base):
                S = BSet()
                t = [r23(base + i * 2048, 1024, BF16) for i in range(10)]
                S.B0, S.B1, S.B2 = t[0], t[1], t[2]
                S.B3 = [t[3], t[4]]
                S.B4 = [t[5], t[6]]
                S.B5 = t[7]
                S.B6 = [t[8], t[9]]
                S.B7 = r23(base + 20 * KB, 512, BF16)
                S.B8 = [r23(base + 21 * KB, 512, BF16), r23(base + 22 * KB, 512, BF16)]
                return S
            bsets = [mkset(56 * KB), mkset(0)]
            for i_, S_ in enumerate(bsets):
                S_.ebl = r23(31 * KB + i_ * 64, 16, F32)
            o_sb = [r23(23 * KB, 512, F32), r23(25 * KB, 512, F32)]
            rstd_t = r23(27 * KB, 512, F32)
            t_t = r23(29 * KB, 512, F32)
            S_all = r23(79 * KB, 9 * 256, F32)
            S_bf = r23(88 * KB, 8 * 256, BF16)
            F0, F1, F2, F3, F4, F5 = Fw
            gkl = Tl(R23b, "R23", Wb + 5 * 4096, 1024, F32, parts=16)

            def transposes(src, dst):
                fns = [(lambda e, b=b: e.transpose(pt_bf[:, b * 128:(b + 1) * 128], src.ap[:, b * 128:(b + 1) * 128], ident.ap))
                       for b in range(8)]
                tr.op('pe', fns, reads=[src, ident], writes=[pt])
                tr.op('act', lambda e: e.activation(out=dst.ap, in_=pt_bf, func=AF.Copy), reads=[pt], writes=[dst])

            def bcast_last(t):
                return t.ap.rearrange("p (c t) -> p c t", t=64)[:, :, 63:64].broadcast_to([128, 16, 64])

            def v3(t):
                return t.ap.rearrange("p (c t) -> p c t", t=64)

            def H(t, hf):
                return t.cols(hf * 512, hf * 512 + 512)

            def v3h(t):
                return t.ap.rearrange("p (c t) -> p c t", t=64)

            def scan_h(dst, src):
                for hf in range(2):
                    d_, s_, c_ = H(dst, hf), H(src, hf), H(cm, hf)
                    tr.op('dve', lambda e, d_=d_, s_=s_, c_=c_: e.tensor_tensor_scan(
                        out=d_.ap, data0=c_.ap, data1=s_.ap, initial=0.0, op0=ALU.mult, op1=ALU.add),
                        reads=[c_, s_], writes=[d_])

            def kdec_and_ebl(S, C=64):
                nch = 512 // C
                for hf in range(2):
                    b1, b0, f3 = H(S.B1, hf), H(S.B0, hf), H(F3, hf)
                    vv = lambda t: t.ap.rearrange("p (c t) -> p c t", t=C)
                    tr.op('dve', lambda e, b1=b1, b0=b0, f3=f3: e.tensor_tensor(
                        out=vv(b1), in0=vv(b0), in1=vv(f3)[:, :, C - 1:C].broadcast_to([128, nch, C]), op=ALU.mult),
                        reads=[b0, f3], writes=[b1])
                    eb = S.ebl.cols(hf * nch, hf * nch + nch)
                    tr.op('dve', lambda e, eb=eb, f3=f3: e.tensor_copy(out=eb.ap.rearrange("p (c o) -> p c o", o=1),
                                                                       in_=vv(f3)[:, :, C - 1:C]),
                          reads=[f3], writes=[eb])

            def cumsum128_fix(t):
                for hf in range(2):
                    th = H(t, hf)
                    v4 = th.ap.rearrange("p (b two t) -> p b two t", two=2, t=64)
                    tr.op('dve', lambda e, v4=v4: e.tensor_tensor(out=v4[:, :, 1, :], in0=v4[:, :, 1, :],
                                                                 in1=v4[:, :, 0, 63:64].broadcast_to([128, 4, 64]), op=ALU.add),
                          reads=[th], writes=[th])

            def tail_transposes(S, ne):
                transposes(S.B1, S.B5)
                for ec in range(ne):
                    transposes(S.B3[ec], S.B6[ec])

            deferred = []

            def flush_deferred():
                while deferred:
                    deferred.pop(0)()

            def tail_gen(S, dv, pers, nw_col, ychunks, C=64):
                ne = dv // 128
                B0, B2, B5, B6, B7, B8 = S.B0, S.B2, S.B5, S.B6, S.B7, S.B8

                nch = 512 // C

                def stage_A(hf):
                    fns = [(lambda e, j=j, blk=4 * hf + j: e.matmul(pa.ap[:, j * 128:(j + 1) * 128],
                                                                    B0.ap[:, blk * 128:(blk + 1) * 128],
                                                                    B2.ap[:, blk * 128:(blk + 1) * 128], start=True, stop=True))
                           for j in range(4)]
                    tr.op('pe', fns, reads=[H(B0, hf), H(B2, hf)], writes=[pa])
                    tt(B7, pa, m2 if C == 64 else m128, ALU.mult)
                    s0 = S_all.cols(0, dv)
                    s_src = pers if hf == 0 else S_all.cols(nch * dv, (nch + 1) * dv)
                    tr.op('dve', lambda e, d_=s0, s_=s_src: e.tensor_copy(out=d_.ap, in_=s_.ap), reads=[s_src], writes=[s0])
                    spb = 512 // dv
                    per_round = 2 * spb
                    banks = [ps, po]
                    nr = nch // per_round
                    for r in range(nr):
                        fns = []
                        rd = []
                        loc = []
                        for k in range(per_round):
                            ci = r * per_round + k
                            c = nch * hf + ci
                            if C == 64:
                                blk, p0, kp = c // 2, 64 * (c % 2), 64
                                bk, sl = banks[c % 2], k // 2
                            else:
                                blk, p0, kp = c, 0, 128
                                bk, sl = banks[k // spb], k % spb
                            loc.append((ci, c, bk, sl))
                            for ec in range(ne):
                                fns.append(lambda e, sl=sl, ec=ec, blk=blk, p0=p0, kp=kp, bk=bk: e.matmul(
                                    bk.ap[:, sl * dv + ec * 128: sl * dv + ec * 128 + 128],
                                    B5.ap[p0:p0 + kp, blk * 128:(blk + 1) * 128],
                                    B6[ec].ap[p0:p0 + kp, blk * 128:(blk + 1) * 128], start=True, stop=True))
                                rd += [B5.cols(blk * 128, blk * 128 + 128), B6[ec].cols(blk * 128, blk * 128 + 128)]
                        tr.op('pe', fns, reads=rd, writes=[ps, po])
                        for (ci, c, bk, sl) in loc:
                            stt(S_all.cols((ci + 1) * dv, (ci + 2) * dv), S_all.cols(ci * dv, (ci + 1) * dv),
                                S.ebl.cols(c, c + 1), bk.cols(sl * dv, (sl + 1) * dv), ALU.mult, ALU.add)
                            if ci == nch // 2 - 1:
                                hh = nch // 2
                                tr.op('dve', lambda e, d_=S_bf.cols(0, hh * dv), s_=S_all.cols(0, hh * dv): e.tensor_copy(out=d_.ap, in_=s_.ap),
                                      reads=[S_all.cols(0, hh * dv)], writes=[S_bf.cols(0, hh * dv)])
                        if r + 1 < nr:
                            yield
                    hh = nch // 2
                    tr.op('dve', lambda e, d_=S_bf.cols(hh * dv, nch * dv), s_=S_all.cols(hh * dv, nch * dv): e.tensor_copy(out=d_.ap, in_=s_.ap),
                          reads=[S_all.cols(hh * dv, nch * dv)], writes=[S_bf.cols(hh * dv, nch * dv)])
                    if hf == 1:
                        tr.op('dve', lambda e, d_=pers, s_=S_all.cols(nch * dv, (nch + 1) * dv): e.tensor_copy(out=d_.ap, in_=s_.ap),
                              reads=[S_all.cols(nch * dv, (nch + 1) * dv)], writes=[pers])

                def stage_O(hf):
                    hh = nch // 2
                    for ec in range(ne):
                        for jh in range(2):
                            fns = []
                            for j in range(2 * jh, 2 * jh + 2):
                                blk = 4 * hf + j
                                fns.append(lambda e, j=j, blk=blk, ec=ec: e.matmul(
                                    pa.ap[:, j * 128:(j + 1) * 128], B6[ec].ap[:, blk * 128:(blk + 1) * 128],
                                    B7.ap[:, j * 128:(j + 1) * 128], start=True, stop=False))
                                npc = 128 // C
                                for cc in range(npc):
                                    ci = j * npc + cc
                                    c0 = blk * 128 + cc * C
                                    fns.append(lambda e, j=j, cc=cc, ci=ci, c0=c0, ec=ec, npc=npc: e.matmul(
                                        pa.ap[:, j * 128 + cc * C: j * 128 + cc * C + C],
                                        S_bf.ap[:, ci * dv + ec * 128: ci * dv + ec * 128 + 128],
                                        B2.ap[:, c0:c0 + C], start=False, stop=(cc == npc - 1)))
                            tr.op('pe', fns, reads=[H(B6[ec], hf), B7.cols(jh * 256, jh * 256 + 256),
                                                    S_bf.cols(jh * hh * dv, (jh + 1) * hh * dv), H(B2, hf)],
                                  writes=[pa.cols(jh * 256, jh * 256 + 256)])
                        tr.op('dve', lambda e, d_=o_sb[ec]: e.tensor_copy(out=d_.ap, in_=pa.ap), reads=[pa], writes=[o_sb[ec]])
                        act(B8[ec], o_sb[ec], AF.Square)

                def stage_N(hf):
                    nb = pt if hf == 0 else ps
                    fns = [(lambda e, ec=ec: e.matmul(nb.ap[:, 0:512], ones.ap, B8[ec].ap, start=(ec == 0), stop=(ec == ne - 1)))
                           for ec in range(ne)]
                    tr.op('pe', fns, reads=[ones] + B8[:ne], writes=[nb])
                    act(rstd_t, nb, AF.Ln, bias=EPS, scale=1.0 / dv)
                    act(rstd_t, rstd_t, AF.Exp, scale=-0.5)

                    def fin():
                        for ec in range(ne):
                            tt(t_t, o_sb[ec], rstd_t, ALU.mult)
                            stt(H(ychunks[ec], hf), t_t, sm(nw_col + ec), H(S.B4[ec], hf), ALU.mult, ALU.mult)
                    return fin

                for _ in stage_A(0):
                    yield
                flush_deferred()
                yield
                stage_O(0)
                yield
                fin0 = stage_N(0)
                for _ in stage_A(1):
                    yield
                fin0()
                yield
                stage_O(1)
                yield
                deferred.append(stage_N(1))
                yield

            def frontA_gen(h, S):
                p = dense(w_in_d[8 + h], hT)
                for hf in range(2):
                    act(H(F0, hf), H(p, hf), AF.Sigmoid)
                for hf in range(2):
                    ts(H(F0, hf), H(F0, hf), omlb.cols(h, h + 1), lb.cols(h, h + 1), ALU.mult, ALU.add)
                for hf in range(2):
                    ts(H(F1, hf), H(F0, hf), -1.0, 1.0, ALU.mult, ALU.add)
                for hf in range(2):
                    act(H(F0, hf), H(F0, hf), AF.Ln)
                yield
                scan_h(F2, F0)
                for hf in range(2):
                    act(H(F3, hf), H(F2, hf), AF.Exp)
                for hf in range(2):
                    act(H(F4, hf), H(F2, hf), AF.Exp, scale=-1.0)
                for hf in range(2):
                    tt(H(S.B0, hf), H(F1, hf), H(F4, hf), ALU.mult)
                kdec_and_ebl(S)
                p = dense(w_in_d[0 + h], hT)
                for hf in range(2):
                    act(H(F5, hf), H(p, hf), AF.Silu)
                for hf in range(2):
                    tt(H(S.B2, hf), H(F5, hf), H(F3, hf), ALU.mult)
                yield
                p = dense(w_in_d[16 + h], hT)
                for hf in range(2):
                    tr.op('dve', lambda e, d_=H(S.B3[0], hf), p_=H(p, hf): e.tensor_copy(out=d_.ap, in_=p_.ap), reads=[H(p, hf)], writes=[H(S.B3[0], hf)])
                yield
                transposes(S.B1, S.B5)
                p = dense(w_in_d[24 + h], hT)
                transposes(S.B3[0], S.B6[0])
                yield
                for hf in range(2):
                    act(H(S.B4[0], hf), H(p, hf), AF.Silu)
                yield

            def gk_low():
                p = nextpd()
                for half in range(2):
                    c0, c1 = half * 512, (half + 1) * 512
                    fns = [(lambda e, kc=kc, c0=c0, c1=c1: e.matmul(p.ap[0:16, c0:c1], wgkl.ap[:, kc * 16:(kc + 1) * 16],
                                                                    hT[kc].ap[:, c0:c1], start=(kc == 0), stop=(kc == 15)))
                           for kc in range(16)]
                    tr.op('pe', fns, reads=[wgkl] + [a.cols(c0, c1) for a in hT], writes=[p.cols(c0, c1)])
                tr.op('act', lambda e: e.activation(out=gkl.ap, in_=p.ap[0:16, :], func=AF.Copy), reads=[p], writes=[gkl])

            def frontB_gen(h, S):
                if h == 0:
                    gk_low()
                p = nextpd()
                for half in range(2):
                    c0, c1 = half * 512, (half + 1) * 512
                    tr.op('pe', lambda e, c0=c0, c1=c1: e.matmul(p.ap[:, c0:c1], wgk2.ap[0:16, h * 128:(h + 1) * 128],
                                                                 gkl.ap[0:16, c0:c1], start=True, stop=True),
                          reads=[wgk2, gkl], writes=[p.cols(c0, c1)])
                for hf in range(2):
                    act(H(F0, hf), H(p, hf), AF.Exp, bias=nbgk.cols(h, h + 1), scale=-1.0)
                for hf in range(2):
                    act(H(F0, hf), H(F0, hf), AF.Ln, bias=1.0)
                yield
                scan_h(F2, F0)
                cumsum128_fix(F2)
                for hf in range(2):
                    act(H(F3, hf), H(F2, hf), AF.Exp, scale=-1.0 / 16.0)
                for hf in range(2):
                    act(H(F4, hf), H(F2, hf), AF.Exp, scale=1.0 / 16.0)
                p = dense(w_in_d[36 + h], hT)
                for hf in range(2):
                    tt(H(S.B0, hf), H(p, hf), H(F4, hf), ALU.mult)
                kdec_and_ebl(S, C=128)
                yield
                p = dense(w_in_d[32 + h], hT)
                for hf in range(2):
                    stt(H(S.B2, hf), H(p, hf), 128.0 ** -0.5, H(F3, hf), ALU.mult, ALU.mult)
                yield
                p = dense(w_in_d[40 + 2 * h], hT)
                for hf in range(2):
                    tr.op('dve', lambda e, d_=H(S.B3[0], hf), p_=H(p, hf): e.tensor_copy(out=d_.ap, in_=p_.ap), reads=[H(p, hf)], writes=[H(S.B3[0], hf)])
                yield
                p = dense(w_in_d[41 + 2 * h], hT)
                for hf in range(2):
                    tr.op('dve', lambda e, d_=H(S.B3[1], hf), p_=H(p, hf): e.tensor_copy(out=d_.ap, in_=p_.ap), reads=[H(p, hf)], writes=[H(S.B3[1], hf)])
                yield
                p = dense(w_in_d[48 + 2 * h], hT)
                for hf in range(2):
                    act(H(S.B4[0], hf), H(p, hf), AF.Silu)
                yield
                transposes(S.B1, S.B5)
                p = dense(w_in_d[49 + 2 * h], hT)
                transposes(S.B3[0], S.B6[0])
                transposes(S.B3[1], S.B6[1])
                yield
                for hf in range(2):
                    act(H(S.B4[1], hf), H(p, hf), AF.Silu)
                yield

            def step(g):
                if g is None:
                    return False
                try:
                    next(g)
                    return True
                except StopIteration:
                    return False

            for s in range(NST):
                tok = slice(s * ST, (s + 1) * ST)
                sqring = [r23(64 * KB, 1024, BF16), r23(66 * KB, 1024, BF16)]
                rs = r23(68 * KB, 1024, F32)
                nbanks = [pt, ps]
                phaseA_scale(s)
                fine_next[0] = True
                if s == 0:
                    debug_dump("hT0", hT[0])
                    chk('A')

                heads = []
                for h in range(8):
                    heads.append(('A', h))
                for h in range(4):
                    heads.append(('B', h))
                tg = None
                for i, (kind, h) in enumerate(heads):
                    S = bsets[(i + 1) % 2]
                    if kind == 'A':
                        fg = frontA_gen(h, S)
                        mk_tail = (lambda S=S, h=h: tail_gen(S, 128, persA[h], SM_ANW, [yT[h]]))
                    else:
                        fg = frontB_gen(h, S)
                        mk_tail = (lambda S=S, h=h: tail_gen(S, 256, persB[h], SM_BNW, [yT[8 + 2 * h], yT[9 + 2 * h]], C=128))
                    while True:
                        a_ = step(tg)
                        b_ = step(fg)
                        if not a_ and not b_:
                            break
                    tg = mk_tail()
                def phaseC_gen(fc):
                    p = dense(w_in_d[56 + fc], hT)
                    act(F0, p, AF.Sigmoid)
                    if s == 0 and fc > 0:
                        ada_some(2 if fc == 1 else 1)
                    yield
                    p = dense(w_upa_d[fc], yT[0:8])
                    tt(F1, p, F0, ALU.mult)
                    if s == 0 and fc > 0:
                        ada_some(2 if fc == 1 else 1)
                    yield
                    p = dense(w_in_d[72 + fc], hT)
                    act(F2, p, AF.Sigmoid)
                    if s == 0 and fc > 0:
                        ada_some(2 if fc == 1 else 1)
                    yield
                    p = dense(w_upb_d[fc], yT[8:16])
                    tt(F3, p, F2, ALU.mult)
                    tt(mg[fc], F1, F3, ALU.add)
                    if s == 0 and fc > 0:
                        ada_some(2 if fc == 1 else 1)
                    yield
                for fc in range(16):
                    cg = phaseC_gen(fc)
                    if fc == 0:
                        step(tg); step(cg)
                        step(tg); step(cg)
                        while step(tg):
                            pass
                        flush_deferred()
                    while step(cg):
                        pass
                if s == 0:
                    debug_dump("yA0", yT[0])
                    debug_dump("yB0", yT[8])
                if s == 0:
                    debug_dump("mg0", mg[0])
                    chk('C')

                for fc in range(16):
                    dma('sp', xres[fc], xT[fc, :, tok], xsem[fc], writes=[xres[fc]])
                    if fc == 0:
                        fine_next[0] = True
                    p = dense(w_o_d[fc], mg)
                    stt(xres[fc], p, gt1(fc), xres[fc], ALU.mult, ALU.add)
                    if fc > 0:
                        ssq_accum(xres[fc - 1], fc - 1, 16, sqring, nbanks)
                ssq_accum(xres[15], 15, 16, sqring, nbanks)
                if s == 0:
                    debug_dump("x1_0", xres[0])
                    chk('D')
                ssq_finish(rs, nbanks)
                tmpr = [r23(72 * KB, 1024, F32), r23(76 * KB, 1024, F32)]
                for fc in range(16):
                    tm = tmpr[fc % 2]
                    tt(tm, xres[fc], rs, ALU.mult)
                    act(mg[fc], tm, AF.Identity, bias=sh2(fc), scale=a2.cols(fc, fc + 1))
                fine_next[0] = True

                actT = [r23(Wb + j * 2048, 1024, BF16) for j in range(11)]
                sgr = [r23(56 * KB, 1024, F32), r23(60 * KB, 1024, F32)]
                for q in range(4):
                    for j in range(11):
                        hc = q * 11 + j
                        sg = sgr[j % 2]
                        p = dense(w_fi_d[hc], mg)
                        act(sg, p, AF.Silu)
                        p = dense(w_fi_d[44 + hc], mg)
                        tt(actT[j], p, sg, ALU.mult)
                    for fc in range(16):
                        if fc == 0:
                            fine_next[0] = True
                        p = dense(w_fo_d[fc, :, q * 1408:(q + 1) * 1408], actT)
                        stt(xres[fc], p, gt2(fc), xres[fc], ALU.mult, ALU.add)
                        if q == 3 and fc > 0:
                            ssq_accum(xres[fc - 1], fc - 1, 16, sqring, nbanks)
                ssq_accum(xres[15], 15, 16, sqring, nbanks)

                if s + 1 < NST:
                    phaseA_loads(s + 1)
                ssq_finish(rs, nbanks)
                for fc in range(16):
                    stt(xres[fc], xres[fc], sm(SM_NF + fc), rs, ALU.mult, ALU.mult)
                    dma('sp', outT[fc, :, tok], xres[fc], osem[fc], reads=[xres[fc]])
                if s + 1 < NST:
                    phaseA_ssq(s + 1)


        except _Stop:
            pass
        for sx in xsem + osem:
            if sx.count > 0:
                tr._wait('sp', sx, sx.count)
        print("instructions:", tr.ninst, "sems:", tr.nsem, "pe:", tr.cnt['pe'], "act:", tr.cnt['act'], "dve:", tr.cnt['dve'])
    return nc


def _colchunks(W):
    K, N = W.shape
    return np.ascontiguousarray(W.reshape(K // 128, 128, N // 128, 128).transpose(2, 1, 0, 3)).reshape(N // 128, 128, (K // 128) * 128)


def _pcol(v):
    return np.ascontiguousarray(v.reshape(-1, 128).T)


def make_consts():
    ident = np.eye(128, dtype=np.float32)
    j = np.arange(128)[:, None]
    i = np.arange(128)[None, :]
    m2 = ((j // 64 == i // 64) & (j <= i)).astype(np.float32)
    m2t = np.tile(m2, (1, 4))
    cmv = np.ones((128, 1024), np.float32)
    cmv[:, ::64] = 0.0
    m128t = np.tile((j <= i).astype(np.float32), (1, 4))
    return np.concatenate([ident, m2t, cmv, m128t], axis=1)


def prep_inputs(x, c, w_ada, b_ada, norm1_w, w_in, w_gk2, b_gk2, lb_param, a_norm_w, b_norm_w,
                w_up_a, w_up_b, w_o, norm2_w, w_ffn_in, w_ffn_out, final_norm_w):
    f = lambda a: np.asarray(a, dtype=np.float32)
    x, c = f(x), f(c)
    w_in0 = f(w_in)[0]
    w_in_ng = np.concatenate([w_in0[:, :7168], w_in0[:, 7184:]], axis=1)
    shared = {
        "wgk2": np.ascontiguousarray(f(w_gk2)[0]),
        "cst": make_consts(),
        "w_ada_l": np.ascontiguousarray(f(w_ada)[0].reshape(4, 4, 128, 24, 512).transpose(3, 0, 2, 1, 4)).reshape(96, 128, 2048),
        "w_in_l": _colchunks(w_in_ng),
        "w_gkl": np.ascontiguousarray(w_in0[:, 7168:7184].reshape(16, 128, 16).transpose(1, 0, 2)).reshape(128, 256),
        "w_upa_l": _colchunks(f(w_up_a)[0]),
        "w_upb_l": _colchunks(f(w_up_b)[0]),
        "w_o_l": _colchunks(f(w_o)[0]),
        "w_fi_l": _colchunks(f(w_ffn_in)[0]),
        "w_fo_l": _colchunks(f(w_ffn_out)[0]),
    }
    sm_common = np.zeros((128, NSM), np.float32)
    sm_common[:, SM_BADA:SM_BADA + 96] = _pcol(f(b_ada)[0])
    sm_common[:, SM_N1:SM_N1 + 16] = _pcol(f(norm1_w)[0])
    sm_common[:, SM_N2:SM_N2 + 16] = _pcol(f(norm2_w)[0])
    sm_common[:, SM_NF:SM_NF + 16] = _pcol(f(final_norm_w))
    sm_common[:, SM_LB0:SM_LB0 + 8] = _pcol(f(lb_param)[0])
    sm_common[:, SM_LB1:SM_LB1 + 8] = _pcol(f(lb_param)[1])
    sm_common[:, SM_ANW:SM_ANW + 1] = _pcol(f(a_norm_w)[0])
    sm_common[:, SM_BNW:SM_BNW + 2] = _pcol(f(b_norm_w)[0])
    sm_common[:, SM_BGK:SM_BGK + 4] = _pcol(f(b_gk2)[0])
    in_maps = []
    for b in range(8):
        smb = sm_common.copy()
        smb[:, SM_CT:SM_CT + 16] = _pcol(c[b])
        m = dict(shared)
        m["smalls"] = smb
        m["xT"] = np.ascontiguousarray(x[b].T).reshape(16, 128, T)
        in_maps.append(m)
    return in_maps


_NC_CACHE = {}


def kernel(**inputs):
    in_maps = prep_inputs(**inputs)
    if "nc" not in _NC_CACHE:
        _NC_CACHE["nc"] = build_nc()
    nc = _NC_CACHE["nc"]
    res = run_bass_kernel_spmd(nc, in_maps, core_ids=list(range(8)))
    out = np.empty((8, T, D), np.float32)
    for b in range(8):
        out[b] = res.results[b]["outT"].reshape(D, T).T
    return out
```
